# Optimizing a Trainium2 kernel written in Bass

```python
import math
import jax
import jax.numpy as jnp
from jax import lax
import numpy as np

D_MODEL = 1024
BATCH = 16
SEQ = 256
DEPTH = 2
DEC_BATCH = 4
DEC_SEQ = 1024
PAST_LEN = 256

GRID_W = 64
D_MIX = 1024
Q_BLK = 128
ROPE_BASE = 10000.0
NORM_EPS = 1e-6
NEG_INF = -1e30
F32 = jnp.float32
W_A = 256
H_A = 4
N_A = 64
LORA_W = 64
LORA_A = 64
GN_EPS = 64e-5
W_B = 256
H_B = 4
KV_B = 2
G_B = 2
HD_B = 64
WINDOW = 128
W_C = 256
NB_C = 4
BS_C = 64
CONV_W = 4
C_RG = 8.0
W_D = 256
H_D = 4
DQ_D = 32
HD_D = 64

PROJ_SIZES = (W_A, W_A, W_A, LORA_W, LORA_A, W_A,
              H_B * HD_B, KV_B * HD_B, KV_B * HD_B, W_B,
              W_C, W_C,
              H_D * 2 * DQ_D, H_D * 2 * DQ_D, H_D * HD_D, W_D)
P_TOTAL = 3456

kernel_name = 'hybrid_diffusion_prefix_step'


def split_cols(p):
    idx = np.cumsum(PROJ_SIZES)[:-1].tolist()
    return jnp.split(p, idx, axis=-1)


def rms_norm(x, g, eps=NORM_EPS):
    xf = x.astype(F32)
    y = xf * lax.rsqrt(jnp.mean(jnp.square(xf), -1, keepdims=True) + eps)
    return (y * g.astype(F32)).astype(x.dtype)


def grid_positions(n_tok):
    rows = n_tok // GRID_W
    row = jnp.repeat(jnp.arange(rows), GRID_W).astype(F32)
    col = jnp.tile(jnp.arange(GRID_W), rows).astype(F32)
    return row, col


def rope_1d(x, pos):
    d = x.shape[-1]
    inv = ROPE_BASE ** (-jnp.arange(0, d, 2, dtype=F32) / d)
    ang = pos[:, None] * inv[None]
    shape = (1, x.shape[1]) + (1,) * (x.ndim - 3) + (d // 2,)
    cos = jnp.cos(ang).reshape(shape)
    sin = jnp.sin(ang).reshape(shape)
    xf = x.astype(F32)
    x1, x2 = xf[..., :d // 2], xf[..., d // 2:]
    return jnp.concatenate([x1 * cos - x2 * sin, x1 * sin + x2 * cos], -1).astype(x.dtype)


def rope_2d(x, row, col):
    h = x.shape[-1] // 2
    return jnp.concatenate([rope_1d(x[..., :h], row), rope_1d(x[..., h:], col)], -1)


def over_query_blocks(fn, q):
    b, t = q.shape[:2]
    nb = t // Q_BLK
    qb = jnp.moveaxis(q.reshape((b, nb, Q_BLK) + q.shape[2:]), 1, 0)
    out = lax.map(lambda a: fn(a[0], a[1]), (jnp.arange(nb), qb))
    return jnp.moveaxis(out, 0, 1).reshape((b, t) + out.shape[3:])


def wkv7_scan(r, w, k, v, kk, a, s0, reverse):
    def step(s, inp):
        r_t, w_t, k_t, v_t, kk_t, a_t = inp
        sa = jnp.einsum('bhij,bhj->bhi', s, -kk_t)
        s = (s * w_t[:, :, None, :] + sa[..., None] * (kk_t * a_t)[:, :, None, :]
             + v_t[..., None] * k_t[:, :, None, :])
        return s, jnp.einsum('bhij,bhj->bhi', s, r_t)
    xs = tuple(jnp.moveaxis(z.astype(F32), 1, 0) for z in (r, w, k, v, kk, a))
    s_fin, y = lax.scan(step, s0.astype(F32), xs, reverse=reverse)
    return s_fin, jnp.moveaxis(y, 0, 1)


def rwkv_branch(r, k, v, wd, ad, lp, s0):
    b, t = r.shape[:2]
    heads = lambda z: z.reshape(b, t, H_A, N_A)
    kk = heads(k * lp['rwkv_k_k']).astype(F32)
    kk = kk * lax.rsqrt(jnp.sum(kk * kk, -1, keepdims=True) + 1e-12)
    wd_t = jnp.tanh(wd)
    ys, finals = [], []
    for d, rev in enumerate((False, True)):
        w_log = -jax.nn.softplus(-(lp['rwkv_w0'][d] + wd_t @ lp['rwkv_w_up'][d]).astype(F32)) - 0.5
        decay = jnp.exp(-jnp.exp(w_log))
        a = jax.nn.sigmoid((lp['rwkv_a0'][d] + ad @ lp['rwkv_a_up'][d]).astype(F32))
        k_d = k.astype(F32) * (1.0 + (a - 1.0) * lp['rwkv_k_a'].astype(F32))
        s_fin, y = wkv7_scan(heads(r), heads(decay), heads(k_d), heads(v), kk, heads(a), s0[:, d], rev)
        ys.append(y)
        finals.append(s_fin)
    y = ys[0] + ys[1]
    mu = jnp.mean(y, -1, keepdims=True)
    var = jnp.mean(jnp.square(y - mu), -1, keepdims=True)
    y = ((y - mu) * lax.rsqrt(var + GN_EPS)).reshape(b, t, W_A) * lp['rwkv_gn_g'].astype(F32) + lp['rwkv_gn_b'].astype(F32)
    bonus = jnp.sum(heads(r).astype(F32) * heads(k).astype(F32) * lp['rwkv_r_k'].astype(F32), -1, keepdims=True) * heads(v).astype(F32)
    y = y + bonus.reshape(b, t, W_A)
    return y.astype(r.dtype), jnp.stack(finals, 1)


def sink_gqa_block(qj, key_sets, sink):
    scale = HD_B ** -0.5
    logits = []
    for k, v, m in key_sets:
        s = jnp.einsum('bqhgd,bkhd->bhgqk', qj, k).astype(F32) * scale
        if m is not None:
            s = jnp.where(m, s, NEG_INF)
        logits.append(s)
    b, q = qj.shape[:2]
    sink_l = jnp.broadcast_to(sink.astype(F32).reshape(1, KV_B, G_B, 1, 1), (b, KV_B, G_B, q, 1))
    p = jax.nn.softmax(jnp.concatenate(logits + [sink_l], -1), -1)
    out, off = None, 0
    for k, v, m in key_sets:
        n = k.shape[1]
        o = jnp.einsum('bhgqk,bkhd->bqhgd', p[..., off:off + n].astype(v.dtype), v)
        out = o if out is None else out + o
        off += n
    return out


def window_latent(q, k, v, ck, cv, sink):
    t = q.shape[1]
    pad = ((0, 0), (Q_BLK, Q_BLK), (0, 0), (0, 0))
    kp = jnp.pad(k, pad)
    vp = jnp.pad(v, pad)

    def blk(j, qj):
        kj = lax.dynamic_slice_in_dim(kp, j * Q_BLK, 3 * Q_BLK, axis=1)
        vj = lax.dynamic_slice_in_dim(vp, j * Q_BLK, 3 * Q_BLK, axis=1)
        qpos = j * Q_BLK + jnp.arange(Q_BLK)
        kpos = (j - 1) * Q_BLK + jnp.arange(3 * Q_BLK)
        valid = (jnp.abs(kpos[None] - qpos[:, None]) <= WINDOW) & (kpos[None] >= 0) & (kpos[None] < t)
        return sink_gqa_block(qj, [(kj, vj, valid), (ck, cv, None)], sink)
    return over_query_blocks(blk, q)


def conv_centred(x, w, bias):
    t = x.shape[1]
    xp = jnp.pad(x, ((0, 0), (CONV_W // 2, CONV_W - 1 - CONV_W // 2), (0, 0)))
    y = bias + xp[:, 0:t] * w[0]
    for i in range(1, CONV_W):
        y = y + xp[:, i:i + t] * w[i]
    return y


def lin_combine(e1, e2):
    a1, b1 = e1
    a2, b2 = e2
    return a1 * a2, a2 * b1 + b2


def rglru_branch(x, lp, h0):
    b, t = x.shape[:2]
    x = conv_centred(x, lp['lru_conv_w'], lp['lru_conv_b'])
    xb = x.reshape(b, t, NB_C, BS_C)
    ys, finals = [], []
    for d, rev in enumerate((False, True)):
        gate_a = jax.nn.sigmoid((jnp.einsum('btnd,nde->btne', xb, lp['lru_wa'][d]).reshape(b, t, W_C) + lp['lru_ba'][d]).astype(F32))
        gate_x = jax.nn.sigmoid((jnp.einsum('btnd,nde->btne', xb, lp['lru_wx'][d]).reshape(b, t, W_C) + lp['lru_bx'][d]).astype(F32))
        log_a = -C_RG * gate_a * jax.nn.softplus(-lp['lru_lambda'][d].astype(F32))
        a = jnp.exp(log_a)
        u = jnp.sqrt(-jnp.expm1(2.0 * log_a)) * (gate_x * x.astype(F32))
        a_cum, h = lax.associative_scan(lin_combine, (a, u), reverse=rev, axis=1)
        h = h + a_cum * h0[:, d][:, None].astype(F32)
        ys.append(h)
        finals.append(h[:, 0] if rev else h[:, -1])
    return (ys[0] + ys[1]).astype(x.dtype), jnp.stack(finals, 1)


def diff_block(qj, key_sets, lam):
    s = jnp.concatenate([jnp.einsum('bqhmd,bkhmd->bhmqk', qj, k) for k, _ in key_sets], -1).astype(F32) * (DQ_D ** -0.5)
    p = jax.nn.softmax(s, -1)
    p = p[:, :, 0] - lam * p[:, :, 1]
    out, off = None, 0
    for k, v in key_sets:
        n = k.shape[1]
        o = jnp.einsum('bhqk,bkhd->bqhd', p[..., off:off + n].astype(v.dtype), v)
        out = o if out is None else out + o
        off += n
    return out


def diff_attend(q, key_sets, lp, lam_init):
    b, t = q.shape[:2]
    lq1, lk1, lq2, lk2 = lp['diff_lambda'].astype(F32)
    lam = jnp.exp(jnp.sum(lq1 * lk1)) - jnp.exp(jnp.sum(lq2 * lk2)) + lam_init
    y = over_query_blocks(lambda j, qj: diff_block(qj, key_sets, lam), q)
    y = rms_norm(y, lp['diff_subln_g']) * (1.0 - lam_init)
    return y.reshape(b, t, W_D)


def mixer(h, lp, lam_init, cache):
    b, t = h.shape[:2]
    (ar, ak, av, awd, aad, ag, bq, bk, bv, bg, cx, cg, dq, dk, dv, dg) = split_cols(h @ lp['w_in'])
    latent = cache is not None
    s0_a = cache['rwkv'] if latent else jnp.zeros((b, 2, H_A, N_A, N_A), F32)
    ya, st_a = rwkv_branch(ar, ak, av, awd, aad, lp, s0_a)
    bq = bq.reshape(b, t, KV_B, G_B, HD_B)
    bk = bk.reshape(b, t, KV_B, HD_B)
    bv = bv.reshape(b, t, KV_B, HD_B)
    dq = dq.reshape(b, t, H_D, 2, DQ_D)
    dk = dk.reshape(b, t, H_D, 2, DQ_D)
    dv = dv.reshape(b, t, H_D, HD_D)
    if latent:
        row, col = grid_positions(t)
        yb = window_latent(rope_2d(bq, row, col), rope_2d(bk, row, col), bv,
                           cache['win_k'], cache['win_v'], lp['win_sink'])
        yd = diff_attend(rope_2d(dq, row, col),
                         [(rope_2d(dk, row, col), dv), (cache['diff_k'], cache['diff_v'])], lp, lam_init)
    else:
        yb = over_query_blocks(lambda j, qj: sink_gqa_block(qj, [(bk, bv, None)], lp['win_sink']), bq)
        yd = diff_attend(dq, [(dk, dv)], lp, lam_init)
    h0_c = cache['lru'] if latent else jnp.zeros((b, 2, W_C), F32)
    yc, st_c = rglru_branch(cx, lp, h0_c)
    y = jnp.concatenate([ya * jax.nn.silu(ag), yb.reshape(b, t, W_B) * jax.nn.silu(bg),
                         yc * jax.nn.silu(cg), yd * jax.nn.silu(dg)], -1) @ lp['w_out']
    new_cache = None if latent else (bk, bv, dk, dv, st_a, st_c)
    return y, new_cache


def layer(x, cvec, lp, lam_init, cache):
    mod = jax.nn.silu(cvec) @ lp['w_mod'] + lp['b_mod']
    shift, scale, gate = jnp.split(mod[:, None, :], 3, -1)
    h = rms_norm(x, lp['g_pre']) * (1.0 + scale) + shift
    y, new_cache = mixer(h, lp, lam_init, cache)
    return x + gate * rms_norm(y, lp['g_post']), new_cache


def setup_inputs(seed: int = 0) -> dict:
    key = jax.random.key(seed)
    ks = iter(jax.random.split(key, 48))
    nrm = lambda shape, s=1.0: jax.random.normal(next(ks), shape, F32) * s
    L = DEPTH
    u = jax.random.uniform(next(ks), (L, 2, W_C), F32, 0.9, 0.999)
    sl = u ** (1.0 / C_RG)
    lru_lambda = jnp.log(sl) - jnp.log1p(-sl)
    rwkv_w0 = jax.random.uniform(next(ks), (L, 2, W_A), F32, -6.0, 1.0)
    return {
        'x_prompt': nrm((BATCH, SEQ, D_MODEL)),
        'x_sample': nrm((DEC_BATCH, DEC_SEQ, D_MODEL)),
        'c': nrm((DEC_BATCH, D_MODEL)),
        'cache_win_k': nrm((DEC_BATCH, L, PAST_LEN, KV_B, HD_B)),
        'cache_win_v': nrm((DEC_BATCH, L, PAST_LEN, KV_B, HD_B)),
        'cache_diff_k': nrm((DEC_BATCH, L, PAST_LEN, H_D, 2, DQ_D)),
        'cache_diff_v': nrm((DEC_BATCH, L, PAST_LEN, H_D, HD_D)),
        'state_rwkv': nrm((DEC_BATCH, L, 2, H_A, N_A, N_A), 0.3),
        'state_lru': nrm((DEC_BATCH, L, 2, W_C), 0.5),
        'c_ctx': nrm((D_MODEL,)),
        'w_mod': nrm((L, D_MODEL, 3 * D_MODEL), 0.5 * D_MODEL ** -0.5),
        'b_mod': nrm((L, 3 * D_MODEL), 0.02),
        'g_pre': 1.0 + nrm((L, D_MODEL), 0.02),
        'g_post': 1.0 + nrm((L, D_MODEL), 0.02),
        'w_in': nrm((L, D_MODEL, P_TOTAL), D_MODEL ** -0.5),
        'w_out': nrm((L, D_MIX, D_MODEL), D_MIX ** -0.5),
        'rwkv_w0': rwkv_w0,
        'rwkv_w_up': nrm((L, 2, LORA_W, W_A), 0.1 * LORA_W ** -0.5),
        'rwkv_a0': nrm((L, 2, W_A), 0.1),
        'rwkv_a_up': nrm((L, 2, LORA_A, W_A), 0.1 * LORA_A ** -0.5),
        'rwkv_k_k': 0.85 + nrm((L, W_A), 0.02),
        'rwkv_k_a': 1.0 + nrm((L, W_A), 0.02),
        'rwkv_r_k': nrm((L, H_A, N_A), 0.1),
        'rwkv_gn_g': 1.0 + nrm((L, W_A), 0.02),
        'rwkv_gn_b': nrm((L, W_A), 0.02),
        'win_sink': nrm((L, H_B), 0.5),
        'lru_conv_w': nrm((L, CONV_W, W_C), CONV_W ** -0.5),
        'lru_conv_b': nrm((L, W_C), 0.02),
        'lru_wa': nrm((L, 2, NB_C, BS_C, BS_C), BS_C ** -0.5),
        'lru_ba': nrm((L, 2, W_C), 0.02),
        'lru_wx': nrm((L, 2, NB_C, BS_C, BS_C), BS_C ** -0.5),
        'lru_bx': nrm((L, 2, W_C), 0.02),
        'lru_lambda': lru_lambda,
        'diff_lambda': nrm((L, 4, DQ_D), 0.1),
        'diff_subln_g': 1.0 + nrm((L, HD_D), 0.02),
    }


def reference(x_prompt, x_sample, c, cache_win_k, cache_win_v, cache_diff_k, cache_diff_v, state_rwkv, state_lru,
              c_ctx, w_mod, b_mod, g_pre, g_post, w_in, w_out,
              rwkv_w0, rwkv_w_up, rwkv_a0, rwkv_a_up, rwkv_k_k, rwkv_k_a, rwkv_r_k, rwkv_gn_g, rwkv_gn_b,
              win_sink, lru_conv_w, lru_conv_b, lru_wa, lru_ba, lru_wx, lru_bx, lru_lambda,
              diff_lambda, diff_subln_g):
    y_p = x_prompt
    y_s = x_sample
    ctx_tensors = []
    for l in range(DEPTH):
        lp = dict(w_mod=w_mod[l], b_mod=b_mod[l], g_pre=g_pre[l], g_post=g_post[l], w_in=w_in[l], w_out=w_out[l],
                  rwkv_w0=rwkv_w0[l], rwkv_w_up=rwkv_w_up[l], rwkv_a0=rwkv_a0[l], rwkv_a_up=rwkv_a_up[l],
                  rwkv_k_k=rwkv_k_k[l], rwkv_k_a=rwkv_k_a[l], rwkv_r_k=rwkv_r_k[l],
                  rwkv_gn_g=rwkv_gn_g[l], rwkv_gn_b=rwkv_gn_b[l], win_sink=win_sink[l],
                  lru_conv_w=lru_conv_w[l], lru_conv_b=lru_conv_b[l], lru_wa=lru_wa[l], lru_ba=lru_ba[l],
                  lru_wx=lru_wx[l], lru_bx=lru_bx[l], lru_lambda=lru_lambda[l],
                  diff_lambda=diff_lambda[l], diff_subln_g=diff_subln_g[l])
        lam_init = 0.8 - 0.6 * math.exp(-0.3 * l)
        y_p, nc = layer(y_p, c_ctx[None], lp, lam_init, None)
        ctx_tensors.append(nc)
        layer_cache = dict(win_k=cache_win_k[:, l], win_v=cache_win_v[:, l], diff_k=cache_diff_k[:, l],
                           diff_v=cache_diff_v[:, l], rwkv=state_rwkv[:, l], lru=state_lru[:, l])
        y_s, _ = layer(y_s, c, lp, lam_init, layer_cache)
    new_win_k = jnp.stack([ct[0] for ct in ctx_tensors], 1)
    new_win_v = jnp.stack([ct[1] for ct in ctx_tensors], 1)
    new_diff_k = jnp.stack([ct[2] for ct in ctx_tensors], 1)
    new_diff_v = jnp.stack([ct[3] for ct in ctx_tensors], 1)
    new_state_rwkv = jnp.stack([ct[4] for ct in ctx_tensors], 1)
    new_state_lru = jnp.stack([ct[5] for ct in ctx_tensors], 1)
    return (y_p, y_s, new_win_k, new_win_v, new_diff_k, new_diff_v, new_state_rwkv, new_state_lru)
```

```python
import math
import os
from contextlib import ExitStack
import numpy as np
import ml_dtypes
import concourse.bass as bass
import concourse.mybir as mybir
from concourse.bass_utils import run_bass_kernel_spmd

F32 = mybir.dt.float32
BF16 = mybir.dt.bfloat16
ALU = mybir.AluOpType
AX = mybir.AxisListType
AF = mybir.ActivationFunctionType

L = 2
DM = 1024
NX = 4352
XA_TM, XA_WDAD, XA_G = 0, 768, 896
XB_Q, XB_QS, XB_K, XB_KS, XB_V, XB_G = 1152, 1408, 1664, 1792, 1920, 2048
XC_X, XC_G = 2304, 2560
XD_Q, XD_QS, XD_K, XD_KS, XD_V, XD_G = 2816, 3072, 3328, 3584, 3840, 4096
R_W0A0 = 0
R_KK, R_KA, R_RK, R_GNG, R_GNB = 1024, 1280, 1536, 1792, 2048
R_SUB, R_DLAM, R_SINK = 0, 256, 384
NR = 388
NCOLP = 22
EPOCH = 30000
RELAXED_SAME_ENGINE = False
NDSEM = 12


class V:
    __slots__ = ("ap", "key")

    def __init__(self, ap, key):
        self.ap = ap
        self.key = key


class TileH:
    def __init__(self, name, ap):
        self.name = name
        self.ap = ap

    def __getitem__(self, idx):
        return V(self.ap[idx], (self.name, None))

    def s(self, sub):
        return _Sub(self, sub)


class _Sub:
    def __init__(self, t, sub):
        self.t = t
        self.sub = sub

    def __getitem__(self, idx):
        return V(self.t.ap[idx], (self.t.name, self.sub))


class Sched:
    ENG = ("pe", "dve", "act", "pool", "sp")

    def __init__(self, nc, es):
        self.nc = nc
        self.es = es
        self.lists = {e: [] for e in self.ENG}
        self.count = {e: 0 for e in self.ENG}
        self.sems = {e: [es.enter_context(nc.semaphore("s_%s_%d" % (e, k))) for k in range(4)] for e in self.ENG}
        self.dsems = {q: [es.enter_context(nc.semaphore("s_dma_%s_%d" % (q, k))) for k in range(NDSEM)] for q in ("sp", "act")}
        self.dcount = {"sp": 0, "act": 0}
        self.dtok = {"sp": [None] * NDSEM, "act": [None] * NDSEM}
        self.waited = {}
        self.res = {}
        self.pending_barrier = {e: [] for e in self.ENG}
        self.last_tok = {}
        self.final_toks = []

    def _need(self, eng, tok, waits):
        if tok is None:
            return
        sem, val, peng, pidx = tok
        if peng == eng and peng != "dma":
            if eng == "pe":
                return
            if RELAXED_SAME_ENGINE and self.count[eng] + 1 - pidx > 1:
                return
        k = (eng, id(sem))
        if self.waited.get(k, 0) >= val:
            return
        self.waited[k] = val
        waits.append((sem, val))

    def _conflicts(self, key):
        name, sub = key
        d = self.res.setdefault(name, {})
        if sub is None:
            return list(d.values()), d
        out = []
        if None in d:
            out.append(d[None])
        if sub in d:
            out.append(d[sub])
        return out, d

    def _deps(self, eng, reads, writes, waits):
        for key in reads:
            ents, _ = self._conflicts(key)
            for ent in ents:
                self._need(eng, ent[0], waits)
                if key[0].startswith("ps"):
                    for t in ent[1].values():
                        if t[2] != eng:
                            self._need(eng, t, waits)
        for key in writes:
            ents, _ = self._conflicts(key)
            for ent in ents:
                self._need(eng, ent[0], waits)
                for t in ent[1].values():
                    self._need(eng, t, waits)

    def _record(self, eng_key, tok, reads, writes):
        for key in reads:
            name, sub = key
            d = self.res.setdefault(name, {})
            ent = d.setdefault(sub, [None, {}])
            ent[1][eng_key] = tok
        for key in writes:
            name, sub = key
            d = self.res.setdefault(name, {})
            if sub is None:
                d.clear()
            d[sub] = [tok, {}]
        self.last_tok[eng_key] = tok

    def op(self, eng, fn, reads=(), writes=()):
        waits = []
        for t in self.pending_barrier[eng]:
            self._need(eng, t, waits)
        self.pending_barrier[eng] = []
        self._deps(eng, reads, writes, waits)
        self.count[eng] += 1
        n = self.count[eng]
        ep = (n - 1) // EPOCH
        sem = self.sems[eng][ep]
        val = n - ep * EPOCH
        tok = (sem, val, eng, n)
        self.lists[eng].append((waits, fn, sem, 1))
        self._record(eng, tok, reads, writes)
        return tok

    def dma(self, q, out, in_, final=False):
        waits = []
        for t in self.pending_barrier[q]:
            self._need(q, t, waits)
        self.pending_barrier[q] = []
        reads = [in_.key]
        writes = [out.key]
        self._deps(q, reads, writes, waits)
        slot = self.dcount[q] % NDSEM
        self._need(q, self.dtok[q][slot], waits)
        val = 16 * (self.dcount[q] // NDSEM + 1)
        self.dcount[q] += 1
        sem = self.dsems[q][slot]
        tok = (sem, val, "dma", self.dcount[q])
        self.dtok[q][slot] = tok
        self.count[q] += 1
        o, i = out.ap, in_.ap
        self.lists[q].append((waits, lambda e: e.dma_start(out=o, in_=i), sem, 16))
        self.count[q] -= 1
        self._record("dma%s%d" % (q, slot), tok, reads, writes)
        if final:
            self.final_toks.append(tok)
        return tok

    def barrier(self):
        toks = [t for t in self.last_tok.values() if t is not None]
        for e in self.ENG:
            self.pending_barrier[e] = list(toks)

    def finish(self, block):
        self.barrier()
        lists = self.lists
        pend = self.pending_barrier
        sched = self

        def emit(eng, handle):
            for waits, fn, sem, inc in lists[eng]:
                for (s, v) in waits:
                    handle.wait_ge(s, v)
                fn(handle).then_inc(sem, inc)
            done = {}
            for t in pend[eng]:
                s, v = t[0], t[1]
                if t[2] == eng and eng != "sp":
                    continue
                if done.get(id(s), 0) < v:
                    done[id(s)] = v
                    handle.wait_ge(s, v)

        @block.tensor
        def _(h):
            emit("pe", h)

        @block.vector
        def _(h):
            emit("dve", h)

        @block.scalar
        def _(h):
            emit("act", h)

        @block.gpsimd
        def _(h):
            emit("pool", h)

        @block.sync
        def _(h):
            emit("sp", h)


class K:
    def __init__(self, S):
        self.S = S

    @staticmethod
    def _a(x):
        return x.ap if isinstance(x, V) else x

    @staticmethod
    def _k(*xs):
        return [x.key for x in xs if isinstance(x, V)]

    def mm(self, out, lhsT, rhs, start=True, stop=True):
        o, l, r = out.ap, lhsT.ap, rhs.ap
        self.S.op("pe", lambda e: e.matmul(o, lhsT=l, rhs=r, start=start, stop=stop),
                  reads=[lhsT.key, rhs.key], writes=[out.key])

    def tt(self, eng, out, a, b, op):
        o, x, y = out.ap, a.ap, b.ap
        self.S.op(eng, lambda e: e.tensor_tensor(out=o, in0=x, in1=y, op=op), reads=[a.key, b.key], writes=[out.key])

    def ts(self, eng, out, a, s1, s2, op0, op1=None):
        o, x, p1, p2 = out.ap, a.ap, self._a(s1), self._a(s2)
        if op1 is None:
            f = lambda e: e.tensor_scalar(out=o, in0=x, scalar1=p1, scalar2=None, op0=op0)
        else:
            f = lambda e: e.tensor_scalar(out=o, in0=x, scalar1=p1, scalar2=p2, op0=op0, op1=op1)
        self.S.op(eng, f, reads=[a.key] + self._k(s1, s2), writes=[out.key])

    def stt(self, out, a, sc, b, op0, op1, accum=None):
        o, x, y, p = out.ap, a.ap, b.ap, self._a(sc)
        if accum is None:
            f = lambda e: e.scalar_tensor_tensor(out=o, in0=x, scalar=p, in1=y, op0=op0, op1=op1)
            w = [out.key]
        else:
            ac = accum.ap
            f = lambda e: e.scalar_tensor_tensor(out=o, in0=x, scalar=p, in1=y, op0=op0, op1=op1, accum_out=ac)
            w = [out.key, accum.key]
        self.S.op("dve", f, reads=[a.key, b.key] + self._k(sc), writes=w)

    def red(self, out, a, op=ALU.add):
        o, x = out.ap, a.ap
        self.S.op("dve", lambda e: e.tensor_reduce(out=o, in_=x, axis=AX.X, op=op), reads=[a.key], writes=[out.key])

    def cp(self, eng, out, a):
        o, x = out.ap, a.ap
        if eng == "act":
            f = lambda e: e.activation(out=o, in_=x, func=AF.Copy)
        else:
            f = lambda e: e.tensor_copy(out=o, in_=x)
        self.S.op(eng, f, reads=[a.key], writes=[out.key])

    def act(self, out, a, func, bias=None, scale=None):
        o, x = out.ap, a.ap
        kw = {}
        if bias is not None:
            kw["bias"] = self._a(bias)
        if scale is not None:
            kw["scale"] = self._a(scale)
        self.S.op("act", lambda e: e.activation(out=o, in_=x, func=func, **kw),
                  reads=[a.key] + self._k(bias, scale), writes=[out.key])

    def recip(self, out, a):
        o, x = out.ap, a.ap
        self.S.op("dve", lambda e: e.reciprocal(out=o, in_=x), reads=[a.key], writes=[out.key])

    def memset(self, eng, out, val):
        o = out.ap
        self.S.op(eng, lambda e: e.memset(o, val), reads=[], writes=[out.key])

    def scan(self, out, d0, d1, init):
        o, x, y, i = out.ap, d0.ap, d1.ap, self._a(init)
        self.S.op("dve", lambda e: e.tensor_tensor_scan(out=o, data0=x, data1=y, initial=i, op0=ALU.mult, op1=ALU.add),
                  reads=[d0.key, d1.key] + self._k(init), writes=[out.key])

    def dma(self, out, in_, q="sp"):
        return self.S.dma(q, out, in_)

    def dmaw(self, out, in_):
        self._wq = 1 - getattr(self, "_wq", 0)
        return self.S.dma(("sp", "act")[self._wq], out, in_)


class Arena:
    def __init__(self, S, ap, words):
        self.S = S
        self.ap = ap
        self.words = words
        self.top = 0
        self.n = 0
        self.peak = 0

    def f(self, n_, pat=None, **kw):
        return self._alloc(n_, F32, pat, kw)

    def b(self, n_, pat=None, **kw):
        return self._alloc(n_, BF16, pat, kw)

    def _alloc(self, n, dt, pat, kw):
        words = n if dt == F32 else (n + 1) // 2
        assert self.top + words <= self.words, ("arena overflow", self.top, words, self.words)
        ap = self.ap[:, self.top:self.top + words]
        if dt == BF16:
            ap = ap.bitcast(BF16)[:, 0:n]
        if pat is not None:
            ap = ap.rearrange(pat, **kw)
        self.top += words
        self.peak = max(self.peak, self.top)
        self.n += 1
        return TileH("ar%d" % self.n, ap)

    def mark(self):
        return self.top

    def release(self, m):
        self.S.barrier()
        self.top = m


def rsqrt_col(k, out, src, mult, eps, tmp):
    k.ts("dve", tmp, src, mult, eps, ALU.mult, ALU.add)
    k.act(tmp, tmp, AF.Sqrt)
    k.recip(out, tmp)


def RV(v, pat, **kw):
    return V(v.ap.rearrange(pat, **kw), v.key)


def BC(v, n):
    g = v.ap.shape[1]
    return V(v.ap.rearrange("p (g o) -> p g o", o=1).to_broadcast([128, g, n]), v.key)


IN_SPECS = [
    ("xp", [2, 256, 1024], F32), ("xs", [1024, 1024], F32), ("cvT", [2, 128, 8], F32),
    ("cwk", [L, 256, 128], F32), ("cwv", [L, 256, 128], F32), ("cdk", [L, 256, 256], F32), ("cdv", [L, 256, 256], F32),
    ("srw", [L, 2, 4, 64, 64], F32), ("slrT", [L, 128, 4], F32),
    ("w_mod", [L, 1024, 3072], F32), ("b_mod", [L, 3072], F32), ("g_pre", [L, 1024], F32), ("g_post", [L, 1024], F32),
    ("w_in_x", [L, 1024, NX], F32), ("w_out", [L, 1024, 1024], F32), ("rows", [L, NR], F32),
    ("wup", [L, 128, 512], F32), ("colp", [L, 128, NCOLP], F32), ("bd", [L, 8, 128, 128], F32),
    ("identf", [128, 128], F32), ("identb", [128, 128], BF16), ("masks", [128, 256], BF16),
    ("ropeB", [128, 2, 1024], F32), ("ropeD", [128, 2, 1024], F32),
    ("colA", [L, 128, 30], F32), ("cmask", [128, 640], F32), ("srwT", [L, 2, 4, 64, 64], F32),
]
OUT_SPECS = [
    ("yp", [2, 256, 1024]), ("ys", [1024, 1024]), ("nwk", [2, L, 256, 128]), ("nwv", [2, L, 256, 128]),
    ("ndk", [2, L, 256, 256]), ("ndv", [2, L, 256, 256]), ("nsr", [2, L, 2, 4, 64, 64]), ("nsl", [2, L, 128, 4]),
]
ARENA_WORDS = 32512
GN_EPS = 64e-5


def build(jobs_sel=(1, 2), nlayers=L, phases="MHACBDO", dbg=None):
    nc = bass.Bass("TRN2", target_bir_lowering=False)
    d = {}
    for name, shape, dt in IN_SPECS:
        d[name] = nc.dram_tensor(name, shape, dt, kind="ExternalInput").ap()
    for name, shape in OUT_SPECS:
        d[name] = nc.dram_tensor(name, shape, F32, kind="ExternalOutput").ap()
    dbg_specs = dbg or []
    for name, n in dbg_specs:
        d[name] = nc.dram_tensor(name, [128, n], F32, kind="ExternalOutput").ap()
    ucount = [0]

    def DV(name, ap):
        ucount[0] += 1
        return V(ap, ("d_" + name, ucount[0]))

    with ExitStack() as es:
        def sb(name, shape, dt):
            return es.enter_context(nc.sbuf_tensor(name, shape, dt))
        HT = TileH("HT", sb("HT", [128, 8, 1024], BF16))
        YCT = TileH("YCT", sb("YCT", [128, 8, 1024], BF16))
        MA1 = TileH("MA1", sb("MA1", [128, 1024], F32))
        MSH = TileH("MSH", sb("MSH", [128, 1024], F32))
        MG2 = TileH("MG2", sb("MG2", [128, 1024], F32))
        ROWS = TileH("ROWS", sb("ROWS", [128, NR], F32))
        IDF = TileH("IDF", sb("IDF", [128, 128], F32))
        IDB = TileH("IDB", sb("IDB", [128, 128], BF16))
        MSK = TileH("MSK", sb("MSK", [128, 256], BF16))
        AR = sb("ARENA", [128, ARENA_WORDS], F32)
        PS = [TileH("ps%d" % i, es.enter_context(nc.psum_tensor("ps%d" % i, [128, 512], F32))) for i in range(8)]
        block = es.enter_context(nc.Block())
        S = Sched(nc, es)
        k = K(S)
        A = Arena(S, AR, ARENA_WORDS)
        psrr = [0]

        def nps():
            psrr[0] = (psrr[0] + 1) % 8
            return PS[psrr[0]]

        k.dma(IDF[:], DV("identf", d["identf"]))
        k.dma(IDB[:], DV("identb", d["identb"]))
        k.dma(MSK[:], DV("masks", d["masks"]))

        def dump(name, v):
            k.dma(DV(name, d[name]), v)

        def load_w(dst, l, col0, n):
            m = A.mark()
            stg = [A.f(4096, "p (c n) -> p c n", n=512) for _ in range(2)]
            src = d["w_in_x"][l].rearrange("(c p) n -> p c n", p=128)
            for bi, j0 in enumerate(range(0, n, 512)):
                nn = min(512, n - j0)
                st = stg[bi % 2]
                k.dmaw(st[:, :, 0:nn], DV("w_in_x", src[:, :, col0 + j0:col0 + j0 + nn]))
                k.cp(("dve", "act")[bi % 2], dst[:, :, j0:j0 + nn], st[:, :, 0:nn])
            A.release(m)

        def phase_mod(cvi, l):
            m0 = A.mark()
            sc = A.f(8)
            crep = A.b(1024, "p (c m) -> p c m", m=128)
            k.dma(sc[:], DV("cvT", d["cvT"][cvi]))
            k.act(sc[:], sc[:], AF.Silu)
            k.cp("dve", crep[:], BC(sc[:], 128))
            bm = A.f(3072)
            gp = A.f(1024)
            gq = A.f(1024)
            k.dma(bm[:], DV("b_mod", d["b_mod"][l:l + 1, :].to_broadcast([128, 3072])))
            k.dma(gp[:], DV("g_pre", d["g_pre"][l:l + 1, :].to_broadcast([128, 1024])))
            k.dma(gq[:], DV("g_post", d["g_post"][l:l + 1, :].to_broadcast([128, 1024])))
            modraw = A.f(3072)
            NSB = 4
            stg = [A.f(1536) for _ in range(NSB)]
            wb = [A.b(1536) for _ in range(NSB)]
            for half in range(2):
                for c in range(8):
                    bi = (half * 8 + c) % NSB
                    k.dmaw(stg[bi][:], DV("w_mod", d["w_mod"][l, c * 128:(c + 1) * 128, half * 1536:(half + 1) * 1536]))
                    k.cp("act" if c % 2 == 0 else "dve", wb[bi][:], stg[bi][:])
                    for j in range(3):
                        k.mm(PS[half * 3 + j][:, :], crep[:, c, :], wb[bi][:, j * 512:(j + 1) * 512], c == 0, c == 7)
                for j in range(3):
                    o = half * 1536 + j * 512
                    k.tt("dve", modraw[:, o:o + 512], PS[half * 3 + j][:, :], bm[:, o:o + 512], ALU.add)
            k.stt(MA1[:], modraw[:, 1024:2048], 1.0, gp[:], ALU.add, ALU.mult)
            k.cp("act", MSH[:], modraw[:, 0:1024])
            k.tt("dve", MG2[:], modraw[:, 2048:3072], gq[:], ALU.mult)
            A.release(m0)

        def xsrc(job, l, i):
            if l == 0:
                return V(job["xin"][i * 128:(i + 1) * 128, :], ("d_xin%d" % job["id"], i))
            return V(job["yout"][i * 128:(i + 1) * 128, :], ("d_yout%d" % job["id"], i))

        def xdst(job, i):
            return V(job["yout"][i * 128:(i + 1) * 128, :], ("d_yout%d" % job["id"], i))

        def phase_h(job, l):
            T = job["T"]
            nt = T // 128
            m0 = A.mark()
            xb = [A.f(1024), A.f(1024)]
            junk = A.f(1024)
            hn = A.f(1024)
            hb = [A.b(1024), A.b(1024)]
            st = A.f(4 * nt)
            for i in range(nt):
                k.dma(xb[i % 2][:], xsrc(job, l, i))
                xi = xb[i % 2][:]
                k.stt(junk[:], xi, 1.0, xi, ALU.mult, ALU.mult, accum=st[:, 4 * i:4 * i + 1])
                rsqrt_col(k, st[:, 4 * i + 1:4 * i + 2], st[:, 4 * i:4 * i + 1], 1.0 / 1024, 1e-6, st[:, 4 * i + 2:4 * i + 3])
                k.stt(hn[:], xi, st[:, 4 * i + 1:4 * i + 2], MA1[:], ALU.mult, ALU.mult)
                k.tt("dve", hb[i % 2][:], hn[:], MSH[:], ALU.add)
                for half in range(2):
                    ps = nps()
                    for cc in range(4):
                        c = half * 4 + cc
                        k.mm(ps[:, cc * 128:(cc + 1) * 128], hb[i % 2][:, c * 128:(c + 1) * 128], IDB[:])
                    k.cp("act", HT[:, half * 4:(half + 1) * 4, i * 128:(i + 1) * 128], RV(ps[:, :], "p (c u) -> p c u", u=128))
            A.release(m0)

        def phase_a(job, l):
            T, latent = job["T"], job["latent"]
            nt, nb = T // 128, T // 64
            m0 = A.mark()
            WA = A.b(8 * 896, "p (c n) -> p c n", n=896)
            load_w(WA, l, XA_TM, 896)
            WUP = A.b(512)
            m1 = A.mark()
            wst = A.f(512)
            k.dma(wst[:], DV("wup", d["wup"][l]))
            k.cp("pool", WUP[:], wst[:])
            A.release(m1)
            Y = A.f(4 * T, "p (g t) -> p g t", t=T)
            Sst = A.f(256)
            mscan = A.mark()
            junk = A.f(256)
            junk2 = junk
            sa = A.f(4)
            Z = A.f(1280)
            Zh = [A.b(1280), A.b(1280)]
            Zl = [A.b(1280), A.b(1280)]
            mix = [A.b(1024, "p (c u) -> p c u", u=128) for _ in range(2)]
            LT = A.b(128)
            kt = A.f(256)
            vm = A.f(256)
            zwa = A.f(512)
            sg = A.f(512)
            t1 = A.f(256)
            t2 = A.f(256)
            ssq = A.f(12)
            vT = [A.f(256, "p (g s) -> p g s", s=64) for _ in range(2)]
            omka = A.f(256)
            k.ts("dve", omka[:], ROWS[:, R_KA:R_KA + 256], -1.0, 1.0, ALU.mult, ALU.add)
            if latent:
                for dd in range(2):
                    k.dma(RV(Sst[dd * 64:(dd + 1) * 64, :], "p (h j) -> p h j", j=64),
                          DV("srw", d["srw"][l, dd].rearrange("h i j -> i h j")))
            else:
                k.memset("dve", Sst[:], 0.0)
            G = lambda g: slice(g * 64, (g + 1) * 64)

            def prep(b):
                mx = mix[b % 2]
                k.cp("pool", mx[:, :, 0:64], HT[:, :, 64 * b:64 * b + 64])
                hi = T - 1 - 64 * b
                lo = hi - 64
                src = HT[:, :, hi:lo:-1] if lo >= 0 else HT[:, :, hi::-1]
                k.cp("dve", mx[:, :, 64:128], src)
                pa, pb = PS[6], PS[7]
                for c in range(8):
                    k.mm(pa[:, 0:512], mx[:, c, :], WA[:, c, 0:512], c == 0, c == 7)
                for c in range(8):
                    k.mm(pb[:, 0:256], mx[:, c, :], WA[:, c, 512:768], c == 0, c == 7)
                for c in range(8):
                    k.mm(pb[:, 256:384], WA[:, c, 768:896], mx[:, c, :], c == 0, c == 7)
                k.cp("act", Z[:, 1024:1280], pa[:, 0:256])
                k.cp("act", kt[:], pa[:, 256:512])
                k.cp("act", vm[:], pb[:, 0:256])
                k.act(LT[0:64, :], pb[0:64, 256:384], AF.Tanh)
                k.cp("act", LT[64:128, :], pb[64:128, 256:384])
                k.mm(pa[:, 0:512], LT[0:64, :], WUP[0:64, :])
                k.mm(pb[:, 0:512], LT[64:128, :], WUP[64:128, :])
                k.tt("dve", zwa[0:64, 0:256], pa[0:64, 0:256], ROWS[0:64, 0:256], ALU.add)
                k.tt("dve", zwa[64:128, 0:256], pa[64:128, 256:512], ROWS[64:128, 512:768], ALU.add)
                k.tt("dve", zwa[0:64, 256:512], pb[0:64, 0:256], ROWS[0:64, 256:512], ALU.add)
                k.tt("dve", zwa[64:128, 256:512], pb[64:128, 256:512], ROWS[64:128, 768:1024], ALU.add)
                k.act(sg[:], zwa[:], AF.Sigmoid)
                k.act(Z[:, 0:256], sg[:, 0:256], AF.Exp, scale=-math.exp(-0.5))
                av = sg[:, 256:512]
                k.tt("dve", t1[:], kt[:], ROWS[:, R_KK:R_KK + 256], ALU.mult)
                k.tt("dve", t2[:], t1[:], t1[:], ALU.mult)
                k.red(ssq[:, 0:4], RV(t2[:], "p (g j) -> p g j", j=64))
                rsqrt_col(k, ssq[:, 4:8], ssq[:, 0:4], 1.0, 1e-12, ssq[:, 8:12])
                k.tt("dve", RV(Z[:, 256:512], "p (g j) -> p g j", j=64), RV(t1[:], "p (g j) -> p g j", j=64),
                     BC(ssq[:, 4:8], 64), ALU.mult)
                k.stt(Z[:, 512:768], Z[:, 256:512], -1.0, av, ALU.mult, ALU.mult)
                k.tt("dve", t2[:], av, ROWS[:, R_KA:R_KA + 256], ALU.mult)
                k.tt("dve", t2[:], t2[:], omka[:], ALU.add)
                k.tt("dve", Z[:, 768:1024], kt[:], t2[:], ALU.mult)
                k.cp("act", Zh[b % 2][:], Z[:])
                k.tt("dve", Zl[b % 2][:], Z[:], Zh[b % 2][:], ALU.subtract)
                for g in range(4):
                    k.mm(pb[0:64, G(g)], vm[0:64, G(g)], IDF[0:64, 0:64])
                    k.mm(pb[64:128, G(g)], vm[64:128, G(g)], IDF[64:128, 64:128])
                k.cp("act", vT[b % 2][:], RV(pb[:, 0:256], "p (g s) -> p g s", s=64))

            def scan_block(b):
                zh, zl, vt = Zh[b % 2], Zl[b % 2], vT[b % 2]
                for s in range(64):
                    buf = (b * 64 + s) % 2
                    P0, P1, P2 = PS[3 * buf], PS[3 * buf + 1], PS[3 * buf + 2]
                    sel = SEL[:, s * 128:(s + 1) * 128]
                    k.mm(P0[:, :], sel, zh[:, 0:512], True, False)
                    k.mm(P0[:, :], sel, zl[:, 0:512], False, True)
                    k.mm(P1[:, :], sel, zh[:, 512:1024], True, False)
                    k.mm(P1[:, :], sel, zl[:, 512:1024], False, True)
                    k.mm(P2[:, 0:256], sel, zh[:, 1024:1280], True, False)
                    k.mm(P2[:, 0:256], sel, zl[:, 1024:1280], False, True)
                    col = b * 64 + s
                    for g in range(4):
                        k.stt(junk.s(g)[:, G(g)], Sst.s(g)[:, G(g)], 1.0, P0[:, 256 + g * 64:320 + g * 64], ALU.mult, ALU.mult,
                              accum=sa.s(g)[:, g:g + 1])
                    for g in range(4):
                        k.tt("dve", Sst.s(g)[:, G(g)], Sst.s(g)[:, G(g)], P0[:, G(g)], ALU.mult)
                    for g in range(4):
                        k.stt(Sst.s(g)[:, G(g)], P1[:, G(g)], sa.s(g)[:, g:g + 1], Sst.s(g)[:, G(g)], ALU.mult, ALU.add)
                    for g in range(4):
                        k.stt(Sst.s(g)[:, G(g)], P1[:, 256 + g * 64:320 + g * 64], vt.s(g)[:, g, s:s + 1], Sst.s(g)[:, G(g)],
                              ALU.mult, ALU.add)
                    for g in range(4):
                        k.stt(junk2.s(g)[:, G(g)], Sst.s(g)[:, G(g)], 1.0, P2[:, G(g)], ALU.mult, ALU.mult,
                              accum=Y.s(g)[:, g, col:col + 1])

            _dbg = os.environ.get("KDBG_A", "")
            prep(0)
            for b in range(nb):
                if b + 1 < nb:
                    prep(b + 1)
                if _dbg != "prep":
                    scan_block(b)
            if _dbg in ("prep", "scan"):
                A.release(m0)
                return
            if not latent and _dbg != "nonsr":
                for dd in range(2):
                    k.dma(DV("nsr", d["nsr"][job["j"], l, dd].rearrange("h i j -> i h j")),
                          RV(Sst[dd * 64:(dd + 1) * 64, :], "p (h j) -> p h j", j=64))
            A.release(mscan)
            WG = A.b(8 * 256, "p (c n) -> p c n", n=256)
            load_w(WG, l, XA_G, 256)
            t2 = A.f(256)
            Yr = A.f(512, "p (g u) -> p g u", u=128)
            E2 = A.f(64)
            k.tt("dve", E2[:], IDF[:, 0:64], IDF[:, 64:128], ALU.add)
            yt = A.f(256)
            yc = A.f(256)
            rt = A.f(256)
            gs = A.f(24)
            gt = A.f(256)
            g3 = lambda v: RV(v, "p (g j) -> p g j", j=64)
            for i in range(nt):
                hi = T - 1 - 128 * i
                lo = hi - 128
                src = Y[:, :, hi:lo:-1] if lo >= 0 else Y[:, :, hi::-1]
                k.cp("dve", Yr[:, :, :], src)
                k.cp("act", Yr[0:64, :, :], Y[0:64, :, i * 128:(i + 1) * 128])
                py = nps()
                for g in range(4):
                    k.mm(py[:, G(g)], Yr[:, g, :], E2[:, :])
                k.cp("act", yt[:], py[:, 0:256])
                k.red(gs[:, 0:4], g3(yt[:]))
                k.ts("dve", gs[:, 4:8], gs[:, 0:4], -1.0 / 64, None, ALU.mult)
                k.tt("dve", g3(yc[:]), g3(yt[:]), BC(gs[:, 4:8], 64), ALU.add)
                k.tt("dve", t2[:], yc[:], yc[:], ALU.mult)
                k.red(gs[:, 8:12], g3(t2[:]))
                rsqrt_col(k, gs[:, 12:16], gs[:, 8:12], 1.0 / 64, GN_EPS, gs[:, 16:20])
                k.tt("dve", g3(yc[:]), g3(yc[:]), BC(gs[:, 12:16], 64), ALU.mult)
                k.tt("dve", yc[:], yc[:], ROWS[:, R_GNG:R_GNG + 256], ALU.mult)
                k.tt("dve", yc[:], yc[:], ROWS[:, R_GNB:R_GNB + 256], ALU.add)
                pr, pr2 = nps(), nps()
                tok = slice(i * 128, (i + 1) * 128)
                for c in range(8):
                    k.mm(pr[:, 0:512], HT[:, c, tok], WA[:, c, 0:512], c == 0, c == 7)
                for c in range(8):
                    k.mm(pr2[:, 0:256], HT[:, c, tok], WA[:, c, 512:768], c == 0, c == 7)
                k.cp("act", rt[:], pr[:, 0:256])
                k.tt("dve", t2[:], rt[:], pr[:, 256:512], ALU.mult)
                k.tt("dve", t2[:], t2[:], ROWS[:, R_RK:R_RK + 256], ALU.mult)
                k.red(gs[:, 20:24], g3(t2[:]))
                k.tt("dve", g3(t2[:]), g3(pr2[:, 0:256]), BC(gs[:, 20:24], 64), ALU.mult)
                k.tt("dve", yc[:], yc[:], t2[:], ALU.add)
                pt, pg = nps(), nps()
                for ct in range(2):
                    k.mm(pt[:, ct * 128:(ct + 1) * 128], yc[:, ct * 128:(ct + 1) * 128], IDF[:, :])
                    for c in range(8):
                        k.mm(pg[:, ct * 128:(ct + 1) * 128], WG[:, c, ct * 128:(ct + 1) * 128], HT[:, c, tok], c == 0, c == 7)
                k.act(gt[:], pg[:, 0:256], AF.Silu)
                k.tt("dve", YCT[:, 0:2, tok], RV(pt[:, 0:256], "p (c u) -> p c u", u=128),
                     RV(gt[:], "p (c u) -> p c u", u=128), ALU.mult)
            A.release(m0)


        def phase_a2(job, l):
            Tt, latent = job["T"], job["latent"]
            BT = 128
            NQB = BT // 64
            m0 = A.mark()
            WA = A.b(8 * 896, "p (c n) -> p c n", n=896)
            load_w(WA, l, XA_TM, 896)
            WUP = A.b(512)
            m1 = A.mark()
            wst = A.f(512)
            k.dma(wst[:], DV("wup", d["wup"][l]))
            k.cp("pool", WUP[:], wst[:])
            A.release(m1)
            CA = A.f(30)
            k.dma(CA[:], DV("colA", d["colA"][l]))
            CM = A.f(640)
            k.dma(CM[:], DV("cmask", d["cmask"]))
            MS, MI, MLt, BO = CM[:, 0:128], CM[:, 128:256], CM[:, 256:384], CM[:, 384:512]
            ZER = CM[:, 576:640]
            Y2 = A.f(4 * Tt, "p (h t) -> p h t", t=Tt)
            mstream = A.mark()

            def mkbuf(dd):
                B = {}
                for nm in ("rT", "kT", "sg", "av", "kk", "t1", "t2", "Wc", "Wi"):
                    B[nm] = A.f(BT)
                B["LT"] = A.b(BT)
                B["hb"] = A.b(8 * BT, "p (c u) -> p c u", u=BT) if dd == 1 else None
                B["KR"] = A.f(NQB * 256, "p (q n) -> p q n", n=256)
                B["NK"] = A.f(NQB * 256, "p (q n) -> p q n", n=256)
                B["C"] = []
                for _ in range(NQB):
                    Cq = dict(AP=A.f(256), BQ=A.f(256), NKt=A.f(256), W=[A.f(384), A.f(384)])
                    Cq["At"] = TileH(Cq["W"][1].name, Cq["W"][1].ap[:, 256:384]) if not os.environ.get("KNOALIAS") else A.f(128)
                    B["C"].append(Cq)
                B["Vt"] = A.f(NQB * 64, "p (q n) -> p q n", n=64)
                B["S"] = A.f(64)
                B["G"] = A.f(64)
                B["U"] = A.f(64)
                B["So"] = A.f(128) if not latent else None
                return B

            def stream(dd, ct, B, sq0, T, jq):
                ps = nps
                nblk = T // BT
                S0 = B["S"]
                if latent:
                    k.dma(S0[:], DV("srwT", d["srwT"][l, dd, 2 * ct:2 * ct + 2].rearrange("h j i -> (h j) i")))
                else:
                    k.memset("dve", S0[:], 0.0)
                k.memset("dve", B["KR"][:], 0.0)
                k.memset("dve", B["NK"][:], 0.0)
                cw0 = CA[:, ct * 6 + dd:ct * 6 + dd + 1]
                ca0 = CA[:, ct * 6 + 2 + dd:ct * 6 + 3 + dd]
                ckk = CA[:, ct * 6 + 4:ct * 6 + 5]
                cka = CA[:, ct * 6 + 5:ct * 6 + 6]
                for blk in range(nblk):
                    b0 = blk * BT
                    if dd == 0:
                        hsrc = lambda c, u0, n: HT[:, c, sq0 + b0 + u0:sq0 + b0 + u0 + n]
                    else:
                        hi = sq0 + T - 1 - b0
                        lo = hi - BT
                        src = HT[:, :, hi:lo:-1] if lo >= 0 else HT[:, :, hi::-1]
                        k.cp("dve", B["hb"][:], src)
                        hsrc = lambda c, u0, n: B["hb"][:, c, u0:u0 + n]
                    p1, p2 = ps(), ps()
                    for c in range(8):
                        k.mm(p1[:, 0:BT], WA[:, c, ct * 128:(ct + 1) * 128], hsrc(c, 0, BT), c == 0, c == 7)
                    for c in range(8):
                        k.mm(p1[:, BT:2 * BT], WA[:, c, 256 + ct * 128:256 + (ct + 1) * 128], hsrc(c, 0, BT), c == 0, c == 7)
                    for c in range(8):
                        k.mm(p2[:, 0:BT], WA[:, c, 768:896], hsrc(c, 0, BT), c == 0, c == 7)
                    k.cp("act", B["rT"][:], p1[:, 0:BT])
                    k.cp("act", B["kT"][:], p1[:, BT:2 * BT])
                    k.act(B["LT"][0:64, :], p2[0:64, 0:BT], AF.Tanh)
                    k.cp("act", B["LT"][64:128, :], p2[64:128, 0:BT])
                    yield
                    wc = dd * 256 + ct * 128
                    p3, p5 = ps(), ps()
                    k.mm(p3[:, 0:BT], WUP[0:64, wc:wc + 128], B["LT"][0:64, :])
                    k.mm(p5[:, 0:BT], WUP[64:128, wc:wc + 128], B["LT"][64:128, :])
                    k.act(B["sg"][:], p3[:, 0:BT], AF.Sigmoid, bias=cw0)
                    k.act(B["av"][:], p5[:, 0:BT], AF.Sigmoid, bias=ca0)
                    k.act(B["sg"][:], B["sg"][:], AF.Exp, scale=-math.exp(-0.5))
                    yield
                    k.ts("dve", B["kk"][:], B["kT"][:], ckk, None, ALU.mult)
                    k.tt("dve", B["t1"][:], B["kk"][:], B["kk"][:], ALU.mult)
                    yield
                    p4 = ps()
                    k.mm(p4[:, 0:BT], BO, B["t1"][:])
                    k.ts("dve", B["t1"][:], p4[:, 0:BT], 1e-12, None, ALU.add)
                    k.act(B["t1"][:], B["t1"][:], AF.Sqrt)
                    k.recip(B["t2"][:], B["t1"][:])
                    k.tt("dve", B["kk"][:], B["kk"][:], B["t2"][:], ALU.mult)
                    yield
                    q3 = lambda v: RV(v, "p (q s) -> p q s", s=64)
                    for q in range(NQB):
                        k.scan(B["Wc"][:, q * 64:(q + 1) * 64], B["sg"][:, q * 64:(q + 1) * 64], ZER, 1.0)
                    k.recip(B["Wi"][:], B["Wc"][:])
                    k.stt(B["t1"][:], B["kk"][:], -1.0, B["av"][:], ALU.mult, ALU.mult)
                    k.ts("dve", B["t2"][:], B["av"][:], -1.0, cka, ALU.add, ALU.mult)
                    k.stt(B["t2"][:], B["t2"][:], 1.0, B["kT"][:], ALU.add, ALU.mult)
                    for hh in range(2):
                        hp = slice(hh * 64, (hh + 1) * 64)
                        cs = slice(hh * 64, (hh + 1) * 64)
                        KRv = B["KR"]
                        NKv = B["NK"]
                        k.tt("dve", KRv[hp, :, hh * 64 + 1:hh * 64 + 64], q3(B["kk"][hp, :])[:, :, 1:64] if False else
                             V(B["kk"].ap[hp, :].rearrange("p (q s) -> p q s", s=64)[:, :, 1:64], B["kk"][:].key),
                             V(B["Wc"].ap[hp, :].rearrange("p (q s) -> p q s", s=64)[:, :, 0:63], B["Wc"][:].key), ALU.mult)
                        k.cp("dve", KRv[hp, :, hh * 64:hh * 64 + 1],
                             V(B["kk"].ap[hp, :].rearrange("p (q s) -> p q s", s=64)[:, :, 0:1], B["kk"][:].key))
                        k.tt("dve", KRv[hp, :, 128 + hh * 64:128 + hh * 64 + 64], q3(B["rT"][hp, :]), q3(B["Wc"][hp, :]), ALU.mult)
                        k.tt("dve", NKv[hp, :, hh * 64:hh * 64 + 64], q3(B["t1"][hp, :]), q3(B["Wi"][hp, :]), ALU.mult)
                        k.tt("dve", NKv[hp, :, 128 + hh * 64:128 + hh * 64 + 64], q3(B["t2"][hp, :]), q3(B["Wi"][hp, :]), ALU.mult)
                    yield
                    KR, NK = B["KR"], B["NK"]
                    pcs = []
                    for q in range(NQB):
                        C = B["C"][q]
                        for hh in range(2):
                            pv = ps()
                            vc = 512 + (2 * ct + hh) * 64
                            for c in range(8):
                                k.mm(pv[hh * 64:(hh + 1) * 64, 0:64], hsrc(c, q * 64, 64), WA[:, c, vc:vc + 64], c == 0, c == 7)
                            k.cp("act", B["Vt"][hh * 64:(hh + 1) * 64, q, :], pv[hh * 64:(hh + 1) * 64, 0:64])
                        pa, pb, pc = ps(), ps(), ps()
                        k.mm(pa[:, 0:256], NK[:, q, 0:128], KR[:, q, :])
                        k.mm(pb[:, 0:256], NK[:, q, 128:256], KR[:, q, :])
                        k.mm(pc[:, 0:128], KR[:, q, 0:128], NK[:, q, 0:128])
                        k.mm(pc[:, 128:256], NK[:, q, 0:128], IDF[:, :])
                        k.mm(pc[:, 256:384], NK[:, q, 128:256], IDF[:, :])
                        k.tt("dve", C["AP"][:, 0:128], pa[:, 0:128], MS, ALU.mult)
                        k.tt("dve", C["AP"][:, 128:256], pa[:, 128:256], MI, ALU.mult)
                        k.tt("dve", C["BQ"][:, 0:128], pb[:, 0:128], MS, ALU.mult)
                        k.tt("dve", C["BQ"][:, 128:256], pb[:, 128:256], MI, ALU.mult)
                        k.tt("dve", C["At"][:], pc[:, 0:128], MLt, ALU.mult)
                        k.cp("act", RV(C["NKt"][:], "p (a n) -> p a n", n=128), RV(pc[:, 128:384], "p (a n) -> p a n", n=128))
                        yield
                    ev = lambda v: V(v.ap.rearrange("p (a n) -> p a n", n=128)[:, 0:3:2, :], v.key)
                    for q in range(NQB):
                        C = B["C"][q]
                        W0 = C["W"][0]
                        px = ps()
                        k.mm(px[:, 0:128], C["At"][:], C["AP"][:, 0:128])
                        k.mm(px[:, 256:384], C["AP"][:, 0:128], C["At"][:])
                        k.tt("dve", W0[:, 128:256], C["AP"][:, 0:128], IDF[:, :], ALU.add)
                        k.cp("act", ev(W0[:, :]), ev(px[:, 0:384]))
                    yield
                    cur = 0
                    for lev in range(1, 6):
                        nxt = 1 - cur
                        last = lev == 5
                        for q in range(NQB):
                            C = B["C"][q]
                            Wc_, Wn = C["W"][cur], C["W"][nxt]
                            px = ps()
                            if last:
                                k.mm(px[:, 128:256], Wc_[:, 256:384], Wc_[:, 128:256])
                            else:
                                k.mm(px[:, 0:256], Wc_[:, 256:384], Wc_[:, 0:256])
                                k.mm(px[:, 256:384], Wc_[:, 0:128], Wc_[:, 256:384])
                                k.cp("act", ev(Wn[:, :]), ev(px[:, 0:384]))
                            k.tt("dve", Wn[:, 128:256], Wc_[:, 128:256], px[:, 128:256], ALU.add)
                        cur = nxt
                        yield
                    for q in range(NQB):
                        C = B["C"][q]
                        Tbd = C["W"][cur][:, 128:256]
                        pg = ps()
                        k.mm(pg[:, 0:64], KR[:, q, 0:128], S0[:], True, False)
                        k.mm(pg[:, 0:64], C["BQ"][:, 0:128], B["Vt"][:, q, :], False, True)
                        k.cp("act", B["G"][:], pg[:, 0:64])
                        yield
                        pu = ps()
                        k.mm(pu[:, 0:64], Tbd, B["G"][:])
                        k.cp("act", B["U"][:], pu[:, 0:64])
                        yield
                        psn = ps()
                        k.mm(psn[:, 0:64], C["NKt"][:, 0:128], B["U"][:], True, False)
                        k.mm(psn[:, 0:64], C["NKt"][:, 128:256], B["Vt"][:, q, :], False, False)
                        k.mm(psn[:, 0:64], IDF[:, :], S0[:], False, True)
                        po = psn[dd * 64:(dd + 1) * 64, 128:256]
                        k.mm(po, S0[:], KR[:, q, 128:256], True, False)
                        k.mm(po, B["U"][:], C["AP"][:, 128:256], False, False)
                        k.mm(po, B["Vt"][:, q, :], C["BQ"][:, 128:256], False, True)
                        k.act(S0[:], psn[:, 0:64], AF.Copy, scale=B["Wc"][:, q * 64 + 63:q * 64 + 64])
                        t0 = b0 + q * 64
                        if dd == 0:
                            dst = Y2[dd * 64:(dd + 1) * 64, 2 * ct:2 * ct + 2, sq0 + t0:sq0 + t0 + 64]
                        else:
                            hi2 = sq0 + T - 1 - t0
                            lo2 = hi2 - 64
                            dst = Y2[64:128, 2 * ct:2 * ct + 2, hi2:lo2:-1] if lo2 >= 0 else Y2[64:128, 2 * ct:2 * ct + 2, hi2::-1]
                        k.cp("dve", dst, RV(po, "p (h t) -> p h t", t=64))
                        yield
                if not latent:
                    pt = ps()
                    k.mm(pt[0:64, 0:128], S0[:], IDF[:, :])
                    k.cp("act", B["So"][0:64, :], pt[0:64, 0:128])
                    for hh in range(2):
                        k.dma(DV("nsr", d["nsr"][jq, l, dd, 2 * ct + hh]), B["So"][0:64, hh * 64:(hh + 1) * 64])

            bufs = [mkbuf(0), mkbuf(1), mkbuf(0), mkbuf(1)]
            for (sq0, Tq, jq) in job["seqs"]:
                gens = [stream(0, 0, bufs[0], sq0, Tq, jq), stream(1, 0, bufs[1], sq0, Tq, jq),
                        stream(0, 1, bufs[2], sq0, Tq, jq), stream(1, 1, bufs[3], sq0, Tq, jq)]
                alive = [True] * 4
                _stop = int(os.environ.get("KDBG_STOP", "0"))
                _n = 0
                while any(alive):
                    for gi, g in enumerate(gens):
                        if alive[gi] and _n >= gi * 3:
                            try:
                                next(g)
                            except StopIteration:
                                alive[gi] = False
                    _n += 1
                    if _stop and _n >= _stop:
                        break
            A.release(mstream)
            if os.environ.get("KDBG_STOP"):
                A.release(m0)
                return
            WG = A.b(8 * 256, "p (c n) -> p c n", n=256)
            load_w(WG, l, XA_G, 256)
            E2 = A.f(64)
            k.tt("dve", E2[:], IDF[:, 0:64], IDF[:, 64:128], ALU.add)
            T = Tt
            NP = min(512, T)
            ys = A.f(NP)
            yc = A.f(NP)
            sq = A.f(NP)
            rr = A.f(NP)
            vv = A.f(NP)
            gg = A.f(NP)
            for n0 in range(0, T, NP):
                for ct in range(2):
                    gcol = lambda q: CA[:, 24 + ct * 3 + q:24 + ct * 3 + q + 1]
                    cs = slice(ct * 128, (ct + 1) * 128)
                    p1 = nps()
                    for hh in range(2):
                        k.mm(p1[hh * 64:(hh + 1) * 64, 0:NP], E2[:], Y2[:, 2 * ct + hh, n0:n0 + NP])
                    k.cp("act", ys[:], p1[:, 0:NP])
                    p2 = nps()
                    k.mm(p2[:, 0:NP], BO, ys[:])
                    k.stt(yc[:], p2[:, 0:NP], -1.0 / 64, ys[:], ALU.mult, ALU.add)
                    k.tt("dve", sq[:], yc[:], yc[:], ALU.mult)
                    p3 = nps()
                    k.mm(p3[:, 0:NP], BO, sq[:])
                    k.ts("dve", sq[:], p3[:, 0:NP], 1.0 / 64, GN_EPS, ALU.mult, ALU.add)
                    k.act(sq[:], sq[:], AF.Sqrt)
                    k.recip(sq[:], sq[:])
                    k.tt("dve", yc[:], yc[:], sq[:], ALU.mult)
                    k.ts("dve", yc[:], yc[:], gcol(0), gcol(1), ALU.mult, ALU.add)
                    pr, pk, pv2, pg2 = nps(), nps(), nps(), nps()
                    for c in range(8):
                        k.mm(pr[:, 0:NP], WA[:, c, ct * 128:(ct + 1) * 128], HT[:, c, n0:n0 + NP], c == 0, c == 7)
                    for c in range(8):
                        k.mm(pk[:, 0:NP], WA[:, c, 256 + ct * 128:256 + (ct + 1) * 128], HT[:, c, n0:n0 + NP], c == 0, c == 7)
                    for c in range(8):
                        k.mm(pv2[:, 0:NP], WA[:, c, 512 + ct * 128:512 + (ct + 1) * 128], HT[:, c, n0:n0 + NP], c == 0, c == 7)
                    for c in range(8):
                        k.mm(pg2[:, 0:NP], WG[:, c, cs], HT[:, c, n0:n0 + NP], c == 0, c == 7)
                    k.cp("act", rr[:], pr[:, 0:NP])
                    k.stt(rr[:], rr[:], gcol(2), pk[:, 0:NP], ALU.mult, ALU.mult)
                    k.cp("act", vv[:], pv2[:, 0:NP])
                    k.act(gg[:], pg2[:, 0:NP], AF.Silu)
                    p4 = nps()
                    k.mm(p4[:, 0:NP], BO, rr[:])
                    k.tt("dve", vv[:], vv[:], p4[:, 0:NP], ALU.mult)
                    k.tt("dve", yc[:], yc[:], vv[:], ALU.add)
                    k.tt("dve", YCT[:, ct, n0:n0 + NP], yc[:], gg[:], ALU.mult)
            A.release(m0)

        def phase_c(job, l):
            T, latent = job["T"], job["latent"]
            m0 = A.mark()
            WC = A.b(8 * 512, "p (c n) -> p c n", n=512)
            load_w(WC, l, XC_X, 512)
            BD = A.b(1024, "p (m q) -> p m q", q=128)
            m1 = A.mark()
            bst = A.f(1024, "p (m q) -> p m q", q=128)
            k.dma(bst[:], DV("bd", d["bd"][l].rearrange("m p q -> p m q")))
            k.cp("pool", BD[:], bst[:])
            A.release(m1)
            CP = A.f(NCOLP)
            k.dma(CP[:], DV("colp", d["colp"][l]))
            c8 = A.f(4)
            cc = A.f(4)
            for ct in range(2):
                k.act(cc[:, ct * 2:ct * 2 + 2], CP[:, ct * 11 + 9:ct * 11 + 11], AF.Exp, scale=-1.0)
            k.act(cc[:], cc[:], AF.Ln, bias=1.0)
            k.ts("dve", c8[:], cc[:], -8.0, None, ALU.mult)
            H0 = A.f(4)
            if latent:
                k.dma(H0[:], DV("slrT", d["slrT"][l]))
            T = job["seqs"][0][1]
            NSs = [A.f(4) for _ in job["seqs"]]
            xT = A.f(T + 4)
            gT = A.f(T)
            xc = A.f(T)
            xcb = A.b(T)
            ga = A.f(T)
            gx = A.f(T)
            aT = A.f(T)
            u = A.f(T)
            hf = A.f(T)
            hbk = A.f(T)
            for (qi_, sq0, ct) in [(qi__, sq[0], ct_) for qi__, sq in enumerate(job["seqs"]) for ct_ in range(2)]:
                NS = NSs[qi_]
                k.memset("dve", xT[:, 0:2], 0.0)
                k.memset("dve", xT[:, T + 2:T + 4], 0.0)
                for tb in range(0, T, 512):
                    n = min(512, T - tb)
                    ps, ps2 = nps(), nps()
                    for c in range(8):
                        k.mm(ps[:, 0:n], WC[:, c, ct * 128:(ct + 1) * 128], HT[:, c, sq0 + tb:sq0 + tb + n], c == 0, c == 7)
                    k.cp("act", xT[:, 2 + tb:2 + tb + n], ps[:, 0:n])
                    for c in range(8):
                        k.mm(ps2[:, 0:n], WC[:, c, 256 + ct * 128:256 + (ct + 1) * 128], HT[:, c, sq0 + tb:sq0 + tb + n], c == 0, c == 7)
                    k.act(gT[:, tb:tb + n], ps2[:, 0:n], AF.Silu)
                cw = lambda q: CP[:, ct * 11 + q:ct * 11 + q + 1]
                k.ts("dve", xc[:], xT[:, 0:T], cw(0), cw(4), ALU.mult, ALU.add)
                for q in range(1, 4):
                    k.stt(xc[:], xT[:, q:q + T], cw(q), xc[:], ALU.mult, ALU.add)
                k.cp("act", xcb[:], xc[:])
                for dd in range(2):
                    for gate, gbuf in ((0, ga), (1, gx)):
                        for tb in range(0, T, 512):
                            n = min(512, T - tb)
                            ps = nps()
                            k.mm(ps[:, 0:n], BD[:, dd * 4 + gate * 2 + ct, :], xcb[:, tb:tb + n])
                            bcol = ct * 11 + (5 if gate == 0 else 7) + dd
                            k.act(gbuf[:, tb:tb + n], ps[:, 0:n], AF.Sigmoid, bias=CP[:, bcol:bcol + 1])
                    k.act(aT[:], ga[:], AF.Exp, scale=c8[:, ct * 2 + dd:ct * 2 + dd + 1])
                    k.tt("dve", u[:], aT[:], aT[:], ALU.mult)
                    k.act(u[:], u[:], AF.Sqrt, bias=1.0, scale=-1.0)
                    k.tt("dve", u[:], u[:], gx[:], ALU.mult)
                    k.tt("dve", u[:], u[:], xc[:], ALU.mult)
                    col = ct * 2 + dd
                    h0 = H0[:, col:col + 1] if latent else 0.0
                    if dd == 0:
                        k.scan(hf[:], aT[:], u[:], h0)
                        k.cp("act", NS[:, col:col + 1], hf[:, T - 1:T])
                    else:
                        k.scan(hbk[:, ::-1], aT[:, ::-1], u[:, ::-1], h0)
                        k.cp("act", NS[:, col:col + 1], hbk[:, 0:1])
                k.tt("dve", hf[:], hf[:], hbk[:], ALU.add)
                k.tt("dve", YCT[:, 4 + ct, sq0:sq0 + T], hf[:], gT[:], ALU.mult)
            if not latent:
                for qi_, sq in enumerate(job["seqs"]):
                    k.dma(DV("nsl", d["nsl"][sq[2], l]), NSs[qi_][:])
            A.release(m0)

        def gate_transpose(ybuf, WGt, goff, chunk0, T):
            nt = T // 128
            gt = A.f(256)
            for i in range(nt):
                tok = slice(i * 128, (i + 1) * 128)
                pt, pg = nps(), nps()
                for ct in range(2):
                    k.mm(pt[:, ct * 128:(ct + 1) * 128], ybuf[:, i, ct * 128:(ct + 1) * 128], IDF[:, :])
                    for c in range(8):
                        k.mm(pg[:, ct * 128:(ct + 1) * 128], WGt[:, c, goff + ct * 128:goff + (ct + 1) * 128], HT[:, c, tok],
                             c == 0, c == 7)
                k.act(gt[:], pg[:, 0:256], AF.Silu)
                k.tt("dve", YCT[:, chunk0:chunk0 + 2, tok], RV(pt[:, 0:256], "p (c u) -> p c u", u=128),
                     RV(gt[:], "p (c u) -> p c u", u=128), ALU.mult)

        def phase_b(job, l):
            T, latent = job["T"], job["latent"]
            nt = T // 128
            nkt = nt + (2 if latent else 0)
            m0 = A.mark()
            WB = A.b(8 * 1152, "p (c n) -> p c n", n=1152)
            load_w(WB, l, XB_Q, 1152)
            qT = A.b(2 * T, "p (g t) -> p g t", t=T)
            kT = A.b(T + 256)
            Va = A.b(nkt * 2 * 65, "p (t h e) -> p t h e", h=2, e=65)
            es_ = A.f(4)
            ybuf = A.f(nt * 256, "p (t n) -> p t n", n=256)
            k.act(es_[:], ROWS[:, R_SINK:R_SINK + 4], AF.Exp)
            k.memset("dve", Va[:, :, :, 64:65], 1.0)
            if latent:
                RB = A.f(2048, "p (a t) -> p a t", t=1024)
                k.dma(RB[:], DV("ropeB", d["ropeB"]))
                ta = A.f(512)
                tb_ = A.f(512)
            for t0 in range(0, T, 512):
                n = min(512, T - t0)
                srcs = [(qT[:, 0, t0:t0 + n], 0, 256), (qT[:, 1, t0:t0 + n], 128, 384), (kT[:, t0:t0 + n], 512, 640)]
                for dst, o1, o2 in srcs:
                    ps = nps()
                    for c in range(8):
                        k.mm(ps[:, 0:n], WB[:, c, o1:o1 + 128], HT[:, c, t0:t0 + n], c == 0, c == 7)
                    if latent:
                        ps2 = nps()
                        for c in range(8):
                            k.mm(ps2[:, 0:n], WB[:, c, o2:o2 + 128], HT[:, c, t0:t0 + n], c == 0, c == 7)
                        k.tt("dve", ta[:, 0:n], ps[:, 0:n], RB[:, 0, t0:t0 + n], ALU.mult)
                        k.tt("dve", tb_[:, 0:n], ps2[:, 0:n], RB[:, 1, t0:t0 + n], ALU.mult)
                        k.tt("dve", dst, ta[:, 0:n], tb_[:, 0:n], ALU.add)
                    else:
                        k.cp("act", dst, ps[:, 0:n])
            kv = [A.f(256), A.f(256)]
            for i in range(nt):
                tok = slice(i * 128, (i + 1) * 128)
                ps = nps()
                for c in range(8):
                    k.mm(ps[:, 0:128], HT[:, c, tok], WB[:, c, 512:640], c == 0, c == 7)
                for c in range(8):
                    k.mm(ps[:, 128:256], HT[:, c, tok], WB[:, c, 768:896], c == 0, c == 7)
                k.cp("act", Va[:, i, :, 0:64], RV(ps[:, 128:256], "p (h e) -> p h e", e=64))
                if not latent:
                    k.cp("act", kv[i % 2][:], ps[:, 0:256])
                    sq0, Tq, jq = [sq for sq in job["seqs"] if sq[0] <= i * 128 < sq[0] + sq[1]][0]
                    tl = slice(i * 128 - sq0, (i + 1) * 128 - sq0)
                    k.dma(DV("nwk", d["nwk"][jq, l, tl, :]), kv[i % 2][:, 0:128])
                    k.dma(DV("nwv", d["nwv"][jq, l, tl, :]), kv[i % 2][:, 128:256])
            if latent:
                for cj in range(2):
                    ck = kv[cj]
                    k.dma(ck[:, 0:128], DV("cwk", d["cwk"][l, cj * 128:(cj + 1) * 128, :]))
                    k.dma(ck[:, 128:256], DV("cwv", d["cwv"][l, cj * 128:(cj + 1) * 128, :]))
                    ps = nps()
                    k.mm(ps[:, 0:128], ck[:, 0:128], IDF[:, :])
                    k.cp("act", kT[:, T + cj * 128:T + (cj + 1) * 128], ps[:, 0:128])
                    k.cp("act", Va[:, nt + cj, :, 0:64], RV(ck[:, 128:256], "p (h e) -> p h e", e=64))
            ETo = A.b(nt * 384, "p (t n) -> p t n", n=384)
            ETc = A.b(2 * T, "p (t n) -> p t n", n=T) if latent else None
            osb = A.f(4)
            for kvh in range(2):
                hs = slice(kvh * 64, (kvh + 1) * 64)
                for g in range(2):
                    h = kvh * 2 + g
                    if latent:
                        for kb in range(nt):
                            qb0, qb1 = max(kb - 1, 0), min(kb + 1, nt - 1)
                            nq = (qb1 - qb0 + 1) * 128
                            s0 = (qb0 - (kb - 1)) * 128
                            ps = nps()
                            k.mm(ps[:, 0:nq], kT[hs, kb * 128:(kb + 1) * 128], qT[hs, g, qb0 * 128:qb0 * 128 + nq])
                            k.act(ETo[:, kb, s0:s0 + nq], ps[:, 0:nq], AF.Exp, scale=0.125)
                            if kb - 1 >= 0:
                                k.tt("pool", ETo[:, kb, 0:128], ETo[:, kb, 0:128], MSK[:, 0:128], ALU.mult)
                            if kb + 1 <= nt - 1:
                                k.tt("pool", ETo[:, kb, 256:384], ETo[:, kb, 256:384], MSK[:, 128:256], ALU.mult)
                        for cj in range(2):
                            for t0 in range(0, T, 512):
                                ps = nps()
                                k.mm(ps[:, 0:512], kT[hs, T + cj * 128:T + (cj + 1) * 128], qT[hs, g, t0:t0 + 512])
                                k.act(ETc[:, cj, t0:t0 + 512], ps[:, 0:512], AF.Exp, scale=0.125)
                    else:
                        for (sq0, Tq, jq) in job["seqs"]:
                            for kb in range(sq0 // 128, (sq0 + Tq) // 128):
                                ps = nps()
                                k.mm(ps[:, 0:Tq], kT[hs, kb * 128:(kb + 1) * 128], qT[hs, g, sq0:sq0 + Tq])
                                k.act(ETo[:, kb, 0:Tq], ps[:, 0:Tq], AF.Exp, scale=0.125)
                    for qi in range(nt):
                        terms = []
                        if latent:
                            for kb in (qi - 1, qi, qi + 1):
                                if 0 <= kb < nt:
                                    sl = (qi - kb + 1) * 128
                                    terms.append((ETo[:, kb, sl:sl + 128], Va[:, kb, kvh, :]))
                            for cj in range(2):
                                terms.append((ETc[:, cj, qi * 128:(qi + 1) * 128], Va[:, nt + cj, kvh, :]))
                        else:
                            sq0, Tq, jq = [sq for sq in job["seqs"] if sq[0] <= qi * 128 < sq[0] + sq[1]][0]
                            ql = qi * 128 - sq0
                            for kb in range(sq0 // 128, (sq0 + Tq) // 128):
                                terms.append((ETo[:, kb, ql:ql + 128], Va[:, kb, kvh, :]))
                        po = nps()
                        for ti, (lt, rv) in enumerate(terms):
                            k.mm(po[:, 0:65], lt, rv, ti == 0, ti == len(terms) - 1)
                        k.tt("dve", osb[:, 0:1], po[:, 64:65], es_[:, h:h + 1], ALU.add)
                        k.recip(osb[:, 1:2], osb[:, 0:1])
                        k.ts("dve", ybuf[:, qi, h * 64:(h + 1) * 64], po[:, 0:64], osb[:, 1:2], None, ALU.mult)
            gate_transpose(ybuf, WB, 896, 2, T)
            A.release(m0)

        def phase_d(job, l):
            T, latent = job["T"], job["latent"]
            nt = T // 128
            nkt = nt + (2 if latent else 0)
            TK = nkt * 128
            lam_init = 0.8 - 0.6 * math.exp(-0.3 * l)
            m0 = A.mark()
            WDq = A.b(8 * 1024, "p (c n) -> p c n", n=1024)
            load_w(WDq, l, XD_Q, 1024)
            WDv = A.b(8 * 512, "p (c n) -> p c n", n=512)
            load_w(WDv, l, XD_V, 512)
            qT = A.b(4 * T, "p (h t) -> p h t", t=T)
            kT = A.b(4 * TK, "p (h t) -> p h t", t=TK)
            Va = A.b(nkt * 4 * 65, "p (t h e) -> p t h e", h=4, e=65)
            ybuf = A.f(nt * 256, "p (t n) -> p t n", n=256)
            k.memset("dve", Va[:, :, :, 64:65], 1.0)
            lm = A.f(64)
            lc = A.f(8)
            k.tt("dve", lm[:, 0:32], ROWS[:, R_DLAM:R_DLAM + 32], ROWS[:, R_DLAM + 32:R_DLAM + 64], ALU.mult)
            k.tt("dve", lm[:, 32:64], ROWS[:, R_DLAM + 64:R_DLAM + 96], ROWS[:, R_DLAM + 96:R_DLAM + 128], ALU.mult)
            k.red(lc[:, 0:2], RV(lm[:], "p (a j) -> p a j", j=32))
            k.act(lc[:, 2:4], lc[:, 0:2], AF.Exp)
            k.tt("dve", lc[:, 4:5], lc[:, 2:3], lc[:, 3:4], ALU.subtract)
            k.ts("dve", lc[:, 5:6], lc[:, 4:5], lam_init, -1.0, ALU.add, ALU.mult)
            mrope = A.mark()
            if latent:
                RD = A.f(2048, "p (a t) -> p a t", t=1024)
                k.dma(RD[:], DV("ropeD", d["ropeD"]))
                ta = A.f(512)
                tb_ = A.f(512)
            for t0 in range(0, T, 512):
                n = min(512, T - t0)
                for h in range(4):
                    for dst, o1, o2 in ((qT[0:64, h, t0:t0 + n], h * 64, 256 + h * 64),
                                        (kT[0:64, h, t0:t0 + n], 512 + h * 64, 768 + h * 64)):
                        ps = nps()
                        for c in range(8):
                            k.mm(ps[0:64, 0:n], WDq[:, c, o1:o1 + 64], HT[:, c, t0:t0 + n], c == 0, c == 7)
                        if latent:
                            ps2 = nps()
                            for c in range(8):
                                k.mm(ps2[0:64, 0:n], WDq[:, c, o2:o2 + 64], HT[:, c, t0:t0 + n], c == 0, c == 7)
                            k.tt("dve", ta[0:64, 0:n], ps[0:64, 0:n], RD[0:64, 0, t0:t0 + n], ALU.mult)
                            k.tt("dve", tb_[0:64, 0:n], ps2[0:64, 0:n], RD[0:64, 1, t0:t0 + n], ALU.mult)
                            k.tt("dve", dst, ta[0:64, 0:n], tb_[0:64, 0:n], ALU.add)
                        else:
                            k.cp("act", dst, ps[0:64, 0:n])
            A.release(mrope)
            kv = [A.f(512), A.f(512)]
            for i in range(nt):
                tok = slice(i * 128, (i + 1) * 128)
                ps = nps()
                for c in range(8):
                    k.mm(ps[:, 0:256], HT[:, c, tok], WDq[:, c, 512:768], c == 0, c == 7)
                for c in range(8):
                    k.mm(ps[:, 256:512], HT[:, c, tok], WDv[:, c, 0:256], c == 0, c == 7)
                k.cp("act", Va[:, i, :, 0:64], RV(ps[:, 256:512], "p (h e) -> p h e", e=64))
                if not latent:
                    k.cp("act", kv[i % 2][:], ps[:, :])
                    sq0, Tq, jq = [sq for sq in job["seqs"] if sq[0] <= i * 128 < sq[0] + sq[1]][0]
                    tl = slice(i * 128 - sq0, (i + 1) * 128 - sq0)
                    k.dma(DV("ndk", d["ndk"][jq, l, tl, :]), kv[i % 2][:, 0:256])
                    k.dma(DV("ndv", d["ndv"][jq, l, tl, :]), kv[i % 2][:, 256:512])
            if latent:
                for cj in range(2):
                    ck = kv[cj]
                    k.dma(ck[:, 0:256], DV("cdk", d["cdk"][l, cj * 128:(cj + 1) * 128, :]))
                    k.dma(ck[:, 256:512], DV("cdv", d["cdv"][l, cj * 128:(cj + 1) * 128, :]))
                    for h in range(4):
                        ps = nps()
                        k.mm(ps[0:64, 0:128], ck[:, h * 64:(h + 1) * 64], IDF[:, :])
                        k.cp("act", kT[0:64, h, T + cj * 128:T + (cj + 1) * 128], ps[0:64, 0:128])
                    k.cp("act", Va[:, nt + cj, :, 0:64], RV(ck[:, 256:512], "p (h e) -> p h e", e=64))
            QC = min(512, T) if latent else 256
            ETs = [A.b(nkt * QC, "p (t n) -> p t n", n=QC) for _ in range(2)]
            o1s = A.f(64)
            dn = A.f(4)
            sc_ = DQ_SCALE
            for h in range(4):
                for q0 in range(0, T, QC):
                    nqt = QC // 128
                    acc = A_yacc
                    if latent:
                        klist = list(range(nkt))
                    else:
                        sq0, Tq, jq = [sq for sq in job["seqs"] if sq[0] <= q0 < sq[0] + sq[1]][0]
                        klist = list(range(sq0 // 128, (sq0 + Tq) // 128))
                    for m in range(2):
                        ms = slice(32 * m, 32 * m + 32)
                        ET = ETs[m]
                        for ki, kb in enumerate(klist):
                            ps = nps()
                            k.mm(ps[:, 0:QC], kT[ms, h, kb * 128:(kb + 1) * 128], qT[ms, h, q0:q0 + QC])
                            k.act(ET[:, ki, :], ps[:, 0:QC], AF.Exp, scale=sc_)
                    for m in range(2):
                        ET = ETs[m]
                        for qi in range(nqt):
                            po = nps()
                            for ki, kb in enumerate(klist):
                                k.mm(po[:, 0:65], ET[:, ki, qi * 128:(qi + 1) * 128], Va[:, kb, h, :], ki == 0, ki == len(klist) - 1)
                            ti = (q0 // 128) + qi
                            ysl = ybuf[:, ti, h * 64:(h + 1) * 64]
                            k.recip(dn[:, m:m + 1], po[:, 64:65])
                            if m == 0:
                                k.ts("dve", acc[:, qi * 64:(qi + 1) * 64], po[:, 0:64], dn[:, 0:1], None, ALU.mult)
                            else:
                                k.tt("dve", dn[:, 2:3], dn[:, 1:2], lc[:, 5:6], ALU.mult)
                                k.stt(ysl, po[:, 0:64], dn[:, 2:3], acc[:, qi * 64:(qi + 1) * 64], ALU.mult, ALU.add)
            sq = A.f(256)
            st = A.f(12)
            g3 = lambda v: RV(v, "p (g j) -> p g j", j=64)
            for i in range(nt):
                yi = ybuf[:, i, :]
                k.tt("dve", sq[:], yi, yi, ALU.mult)
                k.red(st[:, 0:4], g3(sq[:]))
                rsqrt_col(k, st[:, 4:8], st[:, 0:4], 1.0 / 64, 1e-6, st[:, 8:12])
                k.tt("dve", g3(yi), g3(yi), BC(st[:, 4:8], 64), ALU.mult)
                k.stt(yi, yi, 1.0 - lam_init, ROWS[:, R_SUB:R_SUB + 256], ALU.mult, ALU.mult)
            gate_transpose(ybuf, WDv, 256, 6, T)
            A.release(m0)

        def phase_o(job, l):
            T = job["T"]
            nt = T // 128
            m0 = A.mark()
            WO = A.b(8 * 1024, "p (c n) -> p c n", n=1024)
            m1 = A.mark()
            stg = [A.f(2048, "p (c n) -> p c n", n=256) for _ in range(2)]
            src = d["w_out"][l].rearrange("(c p) n -> p c n", p=128)
            for bi in range(4):
                k.dmaw(stg[bi % 2][:], DV("w_out", src[:, :, bi * 256:(bi + 1) * 256]))
                k.cp(("dve", "act")[bi % 2], WO[:, :, bi * 256:(bi + 1) * 256], stg[bi % 2][:])
            A.release(m1)
            yo = A.f(1024)
            junk = A.f(1024)
            xb = [A.f(1024), A.f(1024)]
            st = A.f(4 * nt)
            for i in range(nt):
                tok = slice(i * 128, (i + 1) * 128)
                k.dma(xb[i % 2][:], xsrc(job, l, i))
                pa, pb = nps(), nps()
                for c in range(8):
                    k.mm(pa[:, :], YCT[:, c, tok], WO[:, c, 0:512], c == 0, c == 7)
                for c in range(8):
                    k.mm(pb[:, :], YCT[:, c, tok], WO[:, c, 512:1024], c == 0, c == 7)
                k.cp("act", yo[:, 0:512], pa[:, :])
                k.cp("act", yo[:, 512:1024], pb[:, :])
                k.stt(junk[:], yo[:], 1.0, yo[:], ALU.mult, ALU.mult, accum=st[:, 4 * i:4 * i + 1])
                rsqrt_col(k, st[:, 4 * i + 1:4 * i + 2], st[:, 4 * i:4 * i + 1], 1.0 / 1024, 1e-6, st[:, 4 * i + 2:4 * i + 3])
                k.stt(junk[:], yo[:], st[:, 4 * i + 1:4 * i + 2], MG2[:], ALU.mult, ALU.mult)
                k.tt("dve", xb[i % 2][:], xb[i % 2][:], junk[:], ALU.add)
                k.dma(xdst(job, i), xb[i % 2][:])
            A.release(m0)

        DQ_SCALE = 32 ** -0.5
        A_yacc = None
        xpf = d["xp"].rearrange("a t n -> (a t) n")
        ypf = d["yp"].rearrange("a t n -> (a t) n")
        jobs = [
            dict(id=0, T=256, latent=False, seqs=[(0, 256, 0)], xin=d["xp"][0], yout=d["yp"][0], cv=0),
            dict(id=1, T=512, latent=False, seqs=[(0, 256, 0), (256, 256, 1)], xin=xpf, yout=ypf, cv=0),
            dict(id=2, T=1024, latent=True, seqs=[(0, 1024, 0)], xin=d["xs"], yout=d["ys"], cv=1),
        ]
        for ji in jobs_sel:
            job = jobs[ji]
            T = job["T"]
            nt = T // 128
            for l in range(nlayers):
                k.dma(ROWS[:], DV("rows", d["rows"][l:l + 1, :].to_broadcast([128, NR])))
                if "M" in phases:
                    phase_mod(job["cv"], l)
                if "H" in phases:
                    phase_h(job, l)
                if "A" in phases:
                    phase_a2(job, l)
                if "C" in phases:
                    phase_c(job, l)
                if "B" in phases:
                    phase_b(job, l)
                if "D" in phases:
                    A_yacc_m = A.mark()
                    A_yacc = A.f(256)
                    phase_d(job, l)
                    A.release(A_yacc_m)
                if "O" in phases:
                    phase_o(job, l)
        S.finish(block)
    return nc, A.peak


def _partner(d):
    h = d // 2
    q = h // 2
    idx = np.arange(d)
    p = np.empty(d, np.int64)
    for base in (0, h):
        p[base:base + q] = idx[base + q:base + h]
        p[base + q:base + h] = idx[base:base + q]
    return p


def _rope_tables(d, reps):
    h = d // 2
    q = h // 2
    n = 1024
    row = (np.arange(n) // 64).astype(np.float32)
    col = (np.arange(n) % 64).astype(np.float32)
    inv = (10000.0 ** (-np.arange(0, h, 2, dtype=np.float32) / h)).astype(np.float32)
    cos = np.zeros((d, n), np.float32)
    sin = np.zeros((d, n), np.float32)
    for base, pos in ((0, row), (h, col)):
        ang = (pos[None, :] * inv[:, None]).astype(np.float32)
        c, s_ = np.cos(ang).astype(np.float32), np.sin(ang).astype(np.float32)
        cos[base:base + q] = c
        cos[base + q:base + h] = c
        sin[base:base + q] = -s_
        sin[base + q:base + h] = s_
    return np.tile(cos, (reps, 1)), np.tile(sin, (reps, 1))


def _colidx():
    o = dict(ar=0, ak=256, av=512, awd=768, aad=832, ag=896, bq=1152, bk=1408, bv=1536, bg=1664, cx=1920, cg=2176,
             dq=2432, dk=2688, dv=2944, dg=3200)
    r = lambda a, n: np.arange(a, a + n)
    p64, p32 = _partner(64), _partner(32)
    bq = np.concatenate([o["bq"] + hh * 64 + np.arange(64) for hh in (0, 2, 1, 3)])
    bqs = np.concatenate([o["bq"] + hh * 64 + p64 for hh in (0, 2, 1, 3)])
    bk = r(o["bk"], 128)
    bks = np.concatenate([o["bk"] + hh * 64 + p64 for hh in range(2)])
    dq = r(o["dq"], 256)
    dqs = np.concatenate([o["dq"] + bb * 32 + p32 for bb in range(8)])
    dk = r(o["dk"], 256)
    dks = np.concatenate([o["dk"] + bb * 32 + p32 for bb in range(8)])
    idx = np.concatenate([r(0, 1152), bq, bqs, bk, bks, r(o["bv"], 128), r(o["bg"], 256), r(o["cx"], 256), r(o["cg"], 256),
                          dq, dqs, dk, dks, r(o["dv"], 256), r(o["dg"], 256)])
    assert idx.shape[0] == NX
    return idx


def _consts():
    identf = np.eye(128, dtype=np.float32)
    identb = identf.astype(ml_dtypes.bfloat16)
    sel = np.zeros((128, 64, 128), np.float32)
    for s_ in range(64):
        sel[s_, s_, 0:64] = 1.0
        sel[64 + s_, s_, 64:128] = 1.0
    sel = sel.reshape(128, 8192).astype(ml_dtypes.bfloat16)
    kk, qq = np.meshgrid(np.arange(128), np.arange(128), indexing="ij")
    masks = np.concatenate([(kk <= qq), (kk >= qq)], 1).astype(np.float32).astype(ml_dtypes.bfloat16)
    cb, sb_ = _rope_tables(64, 2)
    cd, sd = _rope_tables(32, 4)
    ropeB = np.ascontiguousarray(np.stack([cb, sb_], 1))
    ropeD = np.ascontiguousarray(np.stack([cd, sd], 1))
    p = np.arange(128)
    hh, ss = p // 64, p % 64
    same = (hh[:, None] == hh[None, :])
    cm = np.zeros((128, 640), np.float32)
    cm[:, 0:128] = same & (ss[:, None] < ss[None, :])
    cm[:, 128:256] = same & (ss[:, None] <= ss[None, :])
    cm[:, 256:384] = same & (ss[None, :] < ss[:, None])
    cm[:, 384:512] = same
    cm[:, 512:576] = 1.0 / 64
    return dict(identf=identf, identb=identb, masks=masks, ropeB=ropeB, ropeD=ropeD, cmask=cm)


def _prep_shared(inp):
    f = lambda a: np.ascontiguousarray(np.asarray(a, dtype=np.float32))
    sh = {}
    sh["w_mod"] = f(inp["w_mod"])
    sh["b_mod"] = f(inp["b_mod"])
    sh["g_pre"] = f(inp["g_pre"])
    sh["g_post"] = f(inp["g_post"])
    sh["w_in_x"] = f(np.asarray(inp["w_in"])[:, :, _colidx()])
    sh["w_out"] = f(inp["w_out"])
    rows = np.zeros((L, NR), np.float32)
    for l in range(L):
        rows[l, R_SUB:R_SUB + 256] = np.tile(np.asarray(inp["diff_subln_g"][l]), 4)
        rows[l, R_DLAM:R_DLAM + 128] = np.asarray(inp["diff_lambda"][l]).reshape(128)
        rows[l, R_SINK:R_SINK + 4] = inp["win_sink"][l]
    sh["rows"] = rows
    wup = np.zeros((L, 128, 512), np.float32)
    colp = np.zeros((L, 128, NCOLP), np.float32)
    bd = np.zeros((L, 8, 128, 128), np.float32)
    for l in range(L):
        for dd in range(2):
            wup[l, 0:64, dd * 256:(dd + 1) * 256] = inp["rwkv_w_up"][l, dd]
            wup[l, 64:128, dd * 256:(dd + 1) * 256] = inp["rwkv_a_up"][l, dd]
        for ct in range(2):
            ch = slice(ct * 128, (ct + 1) * 128)
            for q in range(4):
                colp[l, :, ct * 11 + q] = inp["lru_conv_w"][l, q, ch]
            colp[l, :, ct * 11 + 4] = inp["lru_conv_b"][l, ch]
            for dd in range(2):
                colp[l, :, ct * 11 + 5 + dd] = inp["lru_ba"][l, dd, ch]
                colp[l, :, ct * 11 + 7 + dd] = inp["lru_bx"][l, dd, ch]
                colp[l, :, ct * 11 + 9 + dd] = inp["lru_lambda"][l, dd, ch]
                for gate, wkey in ((0, "lru_wa"), (1, "lru_wx")):
                    for a in range(2):
                        bd[l, dd * 4 + gate * 2 + ct, a * 64:(a + 1) * 64, a * 64:(a + 1) * 64] = inp[wkey][l, dd, 2 * ct + a]
    sh["wup"], sh["colp"], sh["bd"] = wup, colp, bd
    colA = np.zeros((L, 128, 30), np.float32)
    for l in range(L):
        for ct in range(2):
            ch = slice(ct * 128, (ct + 1) * 128)
            for dd in range(2):
                colA[l, :, ct * 6 + dd] = inp["rwkv_w0"][l, dd, ch]
                colA[l, :, ct * 6 + 2 + dd] = inp["rwkv_a0"][l, dd, ch]
            colA[l, :, ct * 6 + 4] = inp["rwkv_k_k"][l, ch]
            colA[l, :, ct * 6 + 5] = inp["rwkv_k_a"][l, ch]
            colA[l, :, 24 + ct * 3 + 0] = inp["rwkv_gn_g"][l, ch]
            colA[l, :, 24 + ct * 3 + 1] = inp["rwkv_gn_b"][l, ch]
            colA[l, :, 24 + ct * 3 + 2] = np.asarray(inp["rwkv_r_k"][l]).reshape(256)[ch]
        for h in range(4):
            hc = slice(h * 64, (h + 1) * 64)
            for half in range(2):
                ps_ = slice(half * 64, (half + 1) * 64)
                colA[l, ps_, 12 + h * 3 + 0] = inp["rwkv_gn_g"][l, hc]
                colA[l, ps_, 12 + h * 3 + 1] = inp["rwkv_gn_b"][l, hc]
                colA[l, ps_, 12 + h * 3 + 2] = np.asarray(inp["rwkv_r_k"][l, h])
    sh["colA"] = colA
    sh.update(_consts())
    return sh


def _core_inputs(inp, sh, core):
    f = lambda a: np.ascontiguousarray(np.asarray(a, dtype=np.float32))
    sbi = core // 2
    m = dict(sh)
    m["xp"] = f(inp["x_prompt"][2 * core:2 * core + 2])
    m["xs"] = f(inp["x_sample"][sbi])
    cv = np.stack([np.asarray(inp["c_ctx"]), np.asarray(inp["c"][sbi])], 0)
    m["cvT"] = f(cv.reshape(2, 8, 128).transpose(0, 2, 1))
    m["cwk"] = f(np.asarray(inp["cache_win_k"][sbi]).reshape(L, 256, 128))
    m["cwv"] = f(np.asarray(inp["cache_win_v"][sbi]).reshape(L, 256, 128))
    m["cdk"] = f(np.asarray(inp["cache_diff_k"][sbi]).reshape(L, 256, 256))
    m["cdv"] = f(np.asarray(inp["cache_diff_v"][sbi]).reshape(L, 256, 256))
    m["srw"] = f(inp["state_rwkv"][sbi])
    m["srwT"] = f(np.asarray(inp["state_rwkv"][sbi]).transpose(0, 1, 2, 4, 3))
    slr = np.asarray(inp["state_lru"][sbi])
    m["slrT"] = f(slr.reshape(L, 2, 2, 128).transpose(0, 3, 2, 1).reshape(L, 128, 4))
    return m


_NC_CACHE = {}


def kernel(**inputs):
    inp = {k_: np.asarray(v) for k_, v in inputs.items()}
    if "nc" not in _NC_CACHE:
        _NC_CACHE["nc"] = build()[0]
    nc = _NC_CACHE["nc"]
    sh = _prep_shared(inp)
    in_maps = [_core_inputs(inp, sh, c) for c in range(8)]
    res = run_bass_kernel_spmd(nc, in_maps, core_ids=list(range(8)))
    R = res.results
    y_p = np.concatenate([R[c]["yp"] for c in range(8)], 0)
    y_s = np.stack([R[2 * b]["ys"] for b in range(4)], 0)
    cat = lambda name: np.concatenate([R[c][name] for c in range(8)], 0)
    nwk = cat("nwk").reshape(16, L, 256, 2, 64)
    nwv = cat("nwv").reshape(16, L, 256, 2, 64)
    ndk = cat("ndk").reshape(16, L, 256, 4, 2, 32)
    ndv = cat("ndv").reshape(16, L, 256, 4, 64)
    nsr = cat("nsr")
    nsl = cat("nsl").reshape(16, L, 128, 2, 2).transpose(0, 1, 4, 3, 2).reshape(16, L, 2, 256)
    out = (y_p, y_s, nwk, nwv, ndk, ndv, nsr, nsl)
    return tuple(np.ascontiguousarray(o, dtype=np.float32) for o in out)
```

```python
import math
import os
from contextlib import ExitStack
import numpy as np
import ml_dtypes
import concourse.bass as bass
import concourse.mybir as mybir
from concourse.bass_utils import run_bass_kernel_spmd

F32 = mybir.dt.float32
BF16 = mybir.dt.bfloat16
ALU = mybir.AluOpType
AX = mybir.AxisListType
AF = mybir.ActivationFunctionType

L = 2
DM = 1024
NX = 4352
XA_TM, XA_WDAD, XA_G = 0, 768, 896
XB_Q, XB_QS, XB_K, XB_KS, XB_V, XB_G = 1152, 1408, 1664, 1792, 1920, 2048
XC_X, XC_G = 2304, 2560
XD_Q, XD_QS, XD_K, XD_KS, XD_V, XD_G = 2816, 3072, 3328, 3584, 3840, 4096
R_W0A0 = 0
R_KK, R_KA, R_RK, R_GNG, R_GNB = 1024, 1280, 1536, 1792, 2048
R_SUB, R_DLAM, R_SINK = 0, 256, 384
NR = 388
NCOLP = 22
EPOCH = 30000
RELAXED_SAME_ENGINE = False
NDSEM = 12


class V:
    __slots__ = ("ap", "key")

    def __init__(self, ap, key):
        self.ap = ap
        self.key = key


class TileH:
    def __init__(self, name, ap):
        self.name = name
        self.ap = ap

    def __getitem__(self, idx):
        return V(self.ap[idx], (self.name, None))

    def s(self, sub):
        return _Sub(self, sub)


class _Sub:
    def __init__(self, t, sub):
        self.t = t
        self.sub = sub

    def __getitem__(self, idx):
        return V(self.t.ap[idx], (self.t.name, self.sub))


class Sched:
    ENG = ("pe", "dve", "act", "pool", "sp")

    def __init__(self, nc, es):
        self.nc = nc
        self.es = es
        self.lists = {e: [] for e in self.ENG}
        self.count = {e: 0 for e in self.ENG}
        self.sems = {e: [es.enter_context(nc.semaphore("s_%s_%d" % (e, k))) for k in range(4)] for e in self.ENG}
        self.dsems = {q: [es.enter_context(nc.semaphore("s_dma_%s_%d" % (q, k))) for k in range(NDSEM)] for q in ("sp", "act")}
        self.dcount = {"sp": 0, "act": 0}
        self.dtok = {"sp": [None] * NDSEM, "act": [None] * NDSEM}
        self.waited = {}
        self.res = {}
        self.pending_barrier = {e: [] for e in self.ENG}
        self.last_tok = {}
        self.final_toks = []

    def _need(self, eng, tok, waits):
        if tok is None:
            return
        sem, val, peng, pidx = tok
        if peng == eng and peng != "dma":
            if eng == "pe":
                return
            if RELAXED_SAME_ENGINE and self.count[eng] + 1 - pidx > 1:
                return
        k = (eng, id(sem))
        if self.waited.get(k, 0) >= val:
            return
        self.waited[k] = val
        waits.append((sem, val))

    def _conflicts(self, key):
        name, sub = key
        d = self.res.setdefault(name, {})
        if sub is None:
            return list(d.values()), d
        out = []
        if None in d:
            out.append(d[None])
        if sub in d:
            out.append(d[sub])
        return out, d

    def _deps(self, eng, reads, writes, waits):
        for key in reads:
            ents, _ = self._conflicts(key)
            for ent in ents:
                self._need(eng, ent[0], waits)
                if key[0].startswith("ps"):
                    for t in ent[1].values():
                        if t[2] != eng:
                            self._need(eng, t, waits)
        for key in writes:
            ents, _ = self._conflicts(key)
            for ent in ents:
                self._need(eng, ent[0], waits)
                for t in ent[1].values():
                    self._need(eng, t, waits)

    def _record(self, eng_key, tok, reads, writes):
        for key in reads:
            name, sub = key
            d = self.res.setdefault(name, {})
            ent = d.setdefault(sub, [None, {}])
            ent[1][eng_key] = tok
        for key in writes:
            name, sub = key
            d = self.res.setdefault(name, {})
            if sub is None:
                d.clear()
            d[sub] = [tok, {}]
        self.last_tok[eng_key] = tok

    def op(self, eng, fn, reads=(), writes=()):
        waits = []
        for t in self.pending_barrier[eng]:
            self._need(eng, t, waits)
        self.pending_barrier[eng] = []
        self._deps(eng, reads, writes, waits)
        self.count[eng] += 1
        n = self.count[eng]
        ep = (n - 1) // EPOCH
        sem = self.sems[eng][ep]
        val = n - ep * EPOCH
        tok = (sem, val, eng, n)
        self.lists[eng].append((waits, fn, sem, 1))
        self._record(eng, tok, reads, writes)
        return tok

    def dma(self, q, out, in_, final=False):
        waits = []
        for t in self.pending_barrier[q]:
            self._need(q, t, waits)
        self.pending_barrier[q] = []
        reads = [in_.key]
        writes = [out.key]
        self._deps(q, reads, writes, waits)
        slot = self.dcount[q] % NDSEM
        self._need(q, self.dtok[q][slot], waits)
        val = 16 * (self.dcount[q] // NDSEM + 1)
        self.dcount[q] += 1
        sem = self.dsems[q][slot]
        tok = (sem, val, "dma", self.dcount[q])
        self.dtok[q][slot] = tok
        self.count[q] += 1
        o, i = out.ap, in_.ap
        self.lists[q].append((waits, lambda e: e.dma_start(out=o, in_=i), sem, 16))
        self.count[q] -= 1
        self._record("dma%s%d" % (q, slot), tok, reads, writes)
        if final:
            self.final_toks.append(tok)
        return tok

    def barrier(self):
        toks = [t for t in self.last_tok.values() if t is not None]
        for e in self.ENG:
            self.pending_barrier[e] = list(toks)

    def finish(self, block):
        self.barrier()
        lists = self.lists
        pend = self.pending_barrier
        sched = self

        def emit(eng, handle):
            for waits, fn, sem, inc in lists[eng]:
                for (s, v) in waits:
                    handle.wait_ge(s, v)
                fn(handle).then_inc(sem, inc)
            done = {}
            for t in pend[eng]:
                s, v = t[0], t[1]
                if t[2] == eng and eng != "sp":
                    continue
                if done.get(id(s), 0) < v:
                    done[id(s)] = v
                    handle.wait_ge(s, v)

        @block.tensor
        def _(h):
            emit("pe", h)

        @block.vector
        def _(h):
            emit("dve", h)

        @block.scalar
        def _(h):
            emit("act", h)

        @block.gpsimd
        def _(h):
            emit("pool", h)

        @block.sync
        def _(h):
            emit("sp", h)


class K:
    def __init__(self, S):
        self.S = S

    @staticmethod
    def _a(x):
        return x.ap if isinstance(x, V) else x

    @staticmethod
    def _k(*xs):
        return [x.key for x in xs if isinstance(x, V)]

    def mm(self, out, lhsT, rhs, start=True, stop=True):
        o, l, r = out.ap, lhsT.ap, rhs.ap
        self.S.op("pe", lambda e: e.matmul(o, lhsT=l, rhs=r, start=start, stop=stop),
                  reads=[lhsT.key, rhs.key], writes=[out.key])

    def tt(self, eng, out, a, b, op):
        o, x, y = out.ap, a.ap, b.ap
        self.S.op(eng, lambda e: e.tensor_tensor(out=o, in0=x, in1=y, op=op), reads=[a.key, b.key], writes=[out.key])

    def ts(self, eng, out, a, s1, s2, op0, op1=None):
        o, x, p1, p2 = out.ap, a.ap, self._a(s1), self._a(s2)
        if op1 is None:
            f = lambda e: e.tensor_scalar(out=o, in0=x, scalar1=p1, scalar2=None, op0=op0)
        else:
            f = lambda e: e.tensor_scalar(out=o, in0=x, scalar1=p1, scalar2=p2, op0=op0, op1=op1)
        self.S.op(eng, f, reads=[a.key] + self._k(s1, s2), writes=[out.key])

    def stt(self, out, a, sc, b, op0, op1, accum=None):
        o, x, y, p = out.ap, a.ap, b.ap, self._a(sc)
        if accum is None:
            f = lambda e: e.scalar_tensor_tensor(out=o, in0=x, scalar=p, in1=y, op0=op0, op1=op1)
            w = [out.key]
        else:
            ac = accum.ap
            f = lambda e: e.scalar_tensor_tensor(out=o, in0=x, scalar=p, in1=y, op0=op0, op1=op1, accum_out=ac)
            w = [out.key, accum.key]
        self.S.op("dve", f, reads=[a.key, b.key] + self._k(sc), writes=w)

    def red(self, out, a, op=ALU.add):
        o, x = out.ap, a.ap
        self.S.op("dve", lambda e: e.tensor_reduce(out=o, in_=x, axis=AX.X, op=op), reads=[a.key], writes=[out.key])

    def cp(self, eng, out, a):
        o, x = out.ap, a.ap
        if eng == "act":
            f = lambda e: e.activation(out=o, in_=x, func=AF.Copy)
        else:
            f = lambda e: e.tensor_copy(out=o, in_=x)
        self.S.op(eng, f, reads=[a.key], writes=[out.key])

    def act(self, out, a, func, bias=None, scale=None):
        o, x = out.ap, a.ap
        kw = {}
        if bias is not None:
            kw["bias"] = self._a(bias)
        if scale is not None:
            kw["scale"] = self._a(scale)
        self.S.op("act", lambda e: e.activation(out=o, in_=x, func=func, **kw),
                  reads=[a.key] + self._k(bias, scale), writes=[out.key])

    def recip(self, out, a):
        o, x = out.ap, a.ap
        self.S.op("dve", lambda e: e.reciprocal(out=o, in_=x), reads=[a.key], writes=[out.key])

    def memset(self, eng, out, val):
        o = out.ap
        self.S.op(eng, lambda e: e.memset(o, val), reads=[], writes=[out.key])

    def scan(self, out, d0, d1, init):
        o, x, y, i = out.ap, d0.ap, d1.ap, self._a(init)
        self.S.op("dve", lambda e: e.tensor_tensor_scan(out=o, data0=x, data1=y, initial=i, op0=ALU.mult, op1=ALU.add),
                  reads=[d0.key, d1.key] + self._k(init), writes=[out.key])

    def dma(self, out, in_, q="sp"):
        return self.S.dma(q, out, in_)

    def dmaw(self, out, in_):
        self._wq = 1 - getattr(self, "_wq", 0)
        return self.S.dma(("sp", "act")[self._wq], out, in_)


class Arena:
    def __init__(self, S, ap, words):
        self.S = S
        self.ap = ap
        self.words = words
        self.top = 0
        self.n = 0
        self.peak = 0

    def f(self, n_, pat=None, **kw):
        return self._alloc(n_, F32, pat, kw)

    def b(self, n_, pat=None, **kw):
        return self._alloc(n_, BF16, pat, kw)

    def _alloc(self, n, dt, pat, kw):
        words = n if dt == F32 else (n + 1) // 2
        assert self.top + words <= self.words, ("arena overflow", self.top, words, self.words)
        ap = self.ap[:, self.top:self.top + words]
        if dt == BF16:
            ap = ap.bitcast(BF16)[:, 0:n]
        if pat is not None:
            ap = ap.rearrange(pat, **kw)
        self.top += words
        self.peak = max(self.peak, self.top)
        self.n += 1
        return TileH("ar%d" % self.n, ap)

    def mark(self):
        return self.top

    def release(self, m):
        self.S.barrier()
        self.top = m


def rsqrt_col(k, out, src, mult, eps, tmp):
    k.ts("dve", tmp, src, mult, eps, ALU.mult, ALU.add)
    k.act(tmp, tmp, AF.Sqrt)
    k.recip(out, tmp)


def RV(v, pat, **kw):
    return V(v.ap.rearrange(pat, **kw), v.key)


def BC(v, n):
    g = v.ap.shape[1]
    return V(v.ap.rearrange("p (g o) -> p g o", o=1).to_broadcast([128, g, n]), v.key)


IN_SPECS = [
    ("xp", [2, 256, 1024], F32), ("xs", [1024, 1024], F32), ("cvT", [2, 128, 8], F32),
    ("cwk", [L, 256, 128], F32), ("cwv", [L, 256, 128], F32), ("cdk", [L, 256, 256], F32), ("cdv", [L, 256, 256], F32),
    ("srw", [L, 2, 4, 64, 64], F32), ("slrT", [L, 128, 4], F32),
    ("w_mod", [L, 1024, 3072], F32), ("b_mod", [L, 3072], F32), ("g_pre", [L, 1024], F32), ("g_post", [L, 1024], F32),
    ("w_in_x", [L, 1024, NX], F32), ("w_out", [L, 1024, 1024], F32), ("rows", [L, NR], F32),
    ("wup", [L, 128, 512], F32), ("colp", [L, 128, NCOLP], F32), ("bd", [L, 8, 128, 128], F32),
    ("identf", [128, 128], F32), ("identb", [128, 128], BF16), ("masks", [128, 256], BF16),
    ("ropeB", [128, 2, 1024], F32), ("ropeD", [128, 2, 1024], F32),
    ("colA", [L, 128, 30], F32), ("cmask", [128, 640], F32), ("srwT", [L, 2, 4, 64, 64], F32),
]
OUT_SPECS = [
    ("yp", [2, 256, 1024]), ("ys", [1024, 1024]), ("nwk", [2, L, 256, 128]), ("nwv", [2, L, 256, 128]),
    ("ndk", [2, L, 256, 256]), ("ndv", [2, L, 256, 256]), ("nsr", [2, L, 2, 4, 64, 64]), ("nsl", [2, L, 128, 4]),
]
ARENA_WORDS = 32512
GN_EPS = 64e-5


def build(jobs_sel=(1, 2), nlayers=L, phases="MHACBDO", dbg=None):
    nc = bass.Bass("TRN2", target_bir_lowering=False)
    d = {}
    for name, shape, dt in IN_SPECS:
        d[name] = nc.dram_tensor(name, shape, dt, kind="ExternalInput").ap()
    for name, shape in OUT_SPECS:
        d[name] = nc.dram_tensor(name, shape, F32, kind="ExternalOutput").ap()
    dbg_specs = dbg or []
    for name, n in dbg_specs:
        d[name] = nc.dram_tensor(name, [128, n], F32, kind="ExternalOutput").ap()
    ucount = [0]

    def DV(name, ap):
        ucount[0] += 1
        return V(ap, ("d_" + name, ucount[0]))

    with ExitStack() as es:
        def sb(name, shape, dt):
            return es.enter_context(nc.sbuf_tensor(name, shape, dt))
        HT = TileH("HT", sb("HT", [128, 8, 1024], BF16))
        YCT = TileH("YCT", sb("YCT", [128, 8, 1024], BF16))
        MA1 = TileH("MA1", sb("MA1", [128, 1024], F32))
        MSH = TileH("MSH", sb("MSH", [128, 1024], F32))
        MG2 = TileH("MG2", sb("MG2", [128, 1024], F32))
        ROWS = TileH("ROWS", sb("ROWS", [128, NR], F32))
        IDF = TileH("IDF", sb("IDF", [128, 128], F32))
        IDB = TileH("IDB", sb("IDB", [128, 128], BF16))
        MSK = TileH("MSK", sb("MSK", [128, 256], BF16))
        AR = sb("ARENA", [128, ARENA_WORDS], F32)
        PS = [TileH("ps%d" % i, es.enter_context(nc.psum_tensor("ps%d" % i, [128, 512], F32))) for i in range(8)]
        block = es.enter_context(nc.Block())
        S = Sched(nc, es)
        k = K(S)
        A = Arena(S, AR, ARENA_WORDS)
        psrr = [0]

        def nps():
            psrr[0] = (psrr[0] + 1) % 8
            return PS[psrr[0]]

        k.dma(IDF[:], DV("identf", d["identf"]))
        k.dma(IDB[:], DV("identb", d["identb"]))
        k.dma(MSK[:], DV("masks", d["masks"]))

        def dump(name, v):
            k.dma(DV(name, d[name]), v)

        def load_w(dst, l, col0, n):
            m = A.mark()
            stg = [A.f(4096, "p (c n) -> p c n", n=512) for _ in range(2)]
            src = d["w_in_x"][l].rearrange("(c p) n -> p c n", p=128)
            for bi, j0 in enumerate(range(0, n, 512)):
                nn = min(512, n - j0)
                st = stg[bi % 2]
                k.dmaw(st[:, :, 0:nn], DV("w_in_x", src[:, :, col0 + j0:col0 + j0 + nn]))
                k.cp(("dve", "act")[bi % 2], dst[:, :, j0:j0 + nn], st[:, :, 0:nn])
            A.release(m)

        def phase_mod(cvi, l):
            m0 = A.mark()
            sc = A.f(8)
            crep = A.b(1024, "p (c m) -> p c m", m=128)
            k.dma(sc[:], DV("cvT", d["cvT"][cvi]))
            k.act(sc[:], sc[:], AF.Silu)
            k.cp("dve", crep[:], BC(sc[:], 128))
            bm = A.f(3072)
            gp = A.f(1024)
            gq = A.f(1024)
            k.dma(bm[:], DV("b_mod", d["b_mod"][l:l + 1, :].to_broadcast([128, 3072])))
            k.dma(gp[:], DV("g_pre", d["g_pre"][l:l + 1, :].to_broadcast([128, 1024])))
            k.dma(gq[:], DV("g_post", d["g_post"][l:l + 1, :].to_broadcast([128, 1024])))
            modraw = A.f(3072)
            NSB = 4
            stg = [A.f(1536) for _ in range(NSB)]
            wb = [A.b(1536) for _ in range(NSB)]
            for half in range(2):
                for c in range(8):
                    bi = (half * 8 + c) % NSB
                    k.dmaw(stg[bi][:], DV("w_mod", d["w_mod"][l, c * 128:(c + 1) * 128, half * 1536:(half + 1) * 1536]))
                    k.cp("act" if c % 2 == 0 else "dve", wb[bi][:], stg[bi][:])
                    for j in range(3):
                        k.mm(PS[half * 3 + j][:, :], crep[:, c, :], wb[bi][:, j * 512:(j + 1) * 512], c == 0, c == 7)
                for j in range(3):
                    o = half * 1536 + j * 512
                    k.tt("dve", modraw[:, o:o + 512], PS[half * 3 + j][:, :], bm[:, o:o + 512], ALU.add)
            k.stt(MA1[:], modraw[:, 1024:2048], 1.0, gp[:], ALU.add, ALU.mult)
            k.cp("act", MSH[:], modraw[:, 0:1024])
            k.tt("dve", MG2[:], modraw[:, 2048:3072], gq[:], ALU.mult)
            A.release(m0)

        def xsrc(job, l, i):
            if l == 0:
                return V(job["xin"][i * 128:(i + 1) * 128, :], ("d_xin%d" % job["id"], i))
            return V(job["yout"][i * 128:(i + 1) * 128, :], ("d_yout%d" % job["id"], i))

        def xdst(job, i):
            return V(job["yout"][i * 128:(i + 1) * 128, :], ("d_yout%d" % job["id"], i))

        def phase_h(job, l):
            T = job["T"]
            nt = T // 128
            m0 = A.mark()
            xb = [A.f(1024), A.f(1024)]
            junk = A.f(1024)
            hn = A.f(1024)
            hb = [A.b(1024), A.b(1024)]
            st = A.f(4 * nt)
            for i in range(nt):
                k.dma(xb[i % 2][:], xsrc(job, l, i))
                xi = xb[i % 2][:]
                k.stt(junk[:], xi, 1.0, xi, ALU.mult, ALU.mult, accum=st[:, 4 * i:4 * i + 1])
                rsqrt_col(k, st[:, 4 * i + 1:4 * i + 2], st[:, 4 * i:4 * i + 1], 1.0 / 1024, 1e-6, st[:, 4 * i + 2:4 * i + 3])
                k.stt(hn[:], xi, st[:, 4 * i + 1:4 * i + 2], MA1[:], ALU.mult, ALU.mult)
                k.tt("dve", hb[i % 2][:], hn[:], MSH[:], ALU.add)
                for half in range(2):
                    ps = nps()
                    for cc in range(4):
                        c = half * 4 + cc
                        k.mm(ps[:, cc * 128:(cc + 1) * 128], hb[i % 2][:, c * 128:(c + 1) * 128], IDB[:])
                    k.cp("act", HT[:, half * 4:(half + 1) * 4, i * 128:(i + 1) * 128], RV(ps[:, :], "p (c u) -> p c u", u=128))
            A.release(m0)

        def phase_a(job, l):
            T, latent = job["T"], job["latent"]
            nt, nb = T // 128, T // 64
            m0 = A.mark()
            WA = A.b(8 * 896, "p (c n) -> p c n", n=896)
            load_w(WA, l, XA_TM, 896)
            WUP = A.b(512)
            m1 = A.mark()
            wst = A.f(512)
            k.dma(wst[:], DV("wup", d["wup"][l]))
            k.cp("pool", WUP[:], wst[:])
            A.release(m1)
            Y = A.f(4 * T, "p (g t) -> p g t", t=T)
            Sst = A.f(256)
            mscan = A.mark()
            junk = A.f(256)
            junk2 = junk
            sa = A.f(4)
            Z = A.f(1280)
            Zh = [A.b(1280), A.b(1280)]
            Zl = [A.b(1280), A.b(1280)]
            mix = [A.b(1024, "p (c u) -> p c u", u=128) for _ in range(2)]
            LT = A.b(128)
            kt = A.f(256)
            vm = A.f(256)
            zwa = A.f(512)
            sg = A.f(512)
            t1 = A.f(256)
            t2 = A.f(256)
            ssq = A.f(12)
            vT = [A.f(256, "p (g s) -> p g s", s=64) for _ in range(2)]
            omka = A.f(256)
            k.ts("dve", omka[:], ROWS[:, R_KA:R_KA + 256], -1.0, 1.0, ALU.mult, ALU.add)
            if latent:
                for dd in range(2):
                    k.dma(RV(Sst[dd * 64:(dd + 1) * 64, :], "p (h j) -> p h j", j=64),
                          DV("srw", d["srw"][l, dd].rearrange("h i j -> i h j")))
            else:
                k.memset("dve", Sst[:], 0.0)
            G = lambda g: slice(g * 64, (g + 1) * 64)

            def prep(b):
                mx = mix[b % 2]
                k.cp("pool", mx[:, :, 0:64], HT[:, :, 64 * b:64 * b + 64])
                hi = T - 1 - 64 * b
                lo = hi - 64
                src = HT[:, :, hi:lo:-1] if lo >= 0 else HT[:, :, hi::-1]
                k.cp("dve", mx[:, :, 64:128], src)
                pa, pb = PS[6], PS[7]
                for c in range(8):
                    k.mm(pa[:, 0:512], mx[:, c, :], WA[:, c, 0:512], c == 0, c == 7)
                for c in range(8):
                    k.mm(pb[:, 0:256], mx[:, c, :], WA[:, c, 512:768], c == 0, c == 7)
                for c in range(8):
                    k.mm(pb[:, 256:384], WA[:, c, 768:896], mx[:, c, :], c == 0, c == 7)
                k.cp("act", Z[:, 1024:1280], pa[:, 0:256])
                k.cp("act", kt[:], pa[:, 256:512])
                k.cp("act", vm[:], pb[:, 0:256])
                k.act(LT[0:64, :], pb[0:64, 256:384], AF.Tanh)
                k.cp("act", LT[64:128, :], pb[64:128, 256:384])
                k.mm(pa[:, 0:512], LT[0:64, :], WUP[0:64, :])
                k.mm(pb[:, 0:512], LT[64:128, :], WUP[64:128, :])
                k.tt("dve", zwa[0:64, 0:256], pa[0:64, 0:256], ROWS[0:64, 0:256], ALU.add)
                k.tt("dve", zwa[64:128, 0:256], pa[64:128, 256:512], ROWS[64:128, 512:768], ALU.add)
                k.tt("dve", zwa[0:64, 256:512], pb[0:64, 0:256], ROWS[0:64, 256:512], ALU.add)
                k.tt("dve", zwa[64:128, 256:512], pb[64:128, 256:512], ROWS[64:128, 768:1024], ALU.add)
                k.act(sg[:], zwa[:], AF.Sigmoid)
                k.act(Z[:, 0:256], sg[:, 0:256], AF.Exp, scale=-math.exp(-0.5))
                av = sg[:, 256:512]
                k.tt("dve", t1[:], kt[:], ROWS[:, R_KK:R_KK + 256], ALU.mult)
                k.tt("dve", t2[:], t1[:], t1[:], ALU.mult)
                k.red(ssq[:, 0:4], RV(t2[:], "p (g j) -> p g j", j=64))
                rsqrt_col(k, ssq[:, 4:8], ssq[:, 0:4], 1.0, 1e-12, ssq[:, 8:12])
                k.tt("dve", RV(Z[:, 256:512], "p (g j) -> p g j", j=64), RV(t1[:], "p (g j) -> p g j", j=64),
                     BC(ssq[:, 4:8], 64), ALU.mult)
                k.stt(Z[:, 512:768], Z[:, 256:512], -1.0, av, ALU.mult, ALU.mult)
                k.tt("dve", t2[:], av, ROWS[:, R_KA:R_KA + 256], ALU.mult)
                k.tt("dve", t2[:], t2[:], omka[:], ALU.add)
                k.tt("dve", Z[:, 768:1024], kt[:], t2[:], ALU.mult)
                k.cp("act", Zh[b % 2][:], Z[:])
                k.tt("dve", Zl[b % 2][:], Z[:], Zh[b % 2][:], ALU.subtract)
                for g in range(4):
                    k.mm(pb[0:64, G(g)], vm[0:64, G(g)], IDF[0:64, 0:64])
                    k.mm(pb[64:128, G(g)], vm[64:128, G(g)], IDF[64:128, 64:128])
                k.cp("act", vT[b % 2][:], RV(pb[:, 0:256], "p (g s) -> p g s", s=64))

            def scan_block(b):
                zh, zl, vt = Zh[b % 2], Zl[b % 2], vT[b % 2]
                for s in range(64):
                    buf = (b * 64 + s) % 2
                    P0, P1, P2 = PS[3 * buf], PS[3 * buf + 1], PS[3 * buf + 2]
                    sel = SEL[:, s * 128:(s + 1) * 128]
                    k.mm(P0[:, :], sel, zh[:, 0:512], True, False)
                    k.mm(P0[:, :], sel, zl[:, 0:512], False, True)
                    k.mm(P1[:, :], sel, zh[:, 512:1024], True, False)
                    k.mm(P1[:, :], sel, zl[:, 512:1024], False, True)
                    k.mm(P2[:, 0:256], sel, zh[:, 1024:1280], True, False)
                    k.mm(P2[:, 0:256], sel, zl[:, 1024:1280], False, True)
                    col = b * 64 + s
                    for g in range(4):
                        k.stt(junk.s(g)[:, G(g)], Sst.s(g)[:, G(g)], 1.0, P0[:, 256 + g * 64:320 + g * 64], ALU.mult, ALU.mult,
                              accum=sa.s(g)[:, g:g + 1])
                    for g in range(4):
                        k.tt("dve", Sst.s(g)[:, G(g)], Sst.s(g)[:, G(g)], P0[:, G(g)], ALU.mult)
                    for g in range(4):
                        k.stt(Sst.s(g)[:, G(g)], P1[:, G(g)], sa.s(g)[:, g:g + 1], Sst.s(g)[:, G(g)], ALU.mult, ALU.add)
                    for g in range(4):
                        k.stt(Sst.s(g)[:, G(g)], P1[:, 256 + g * 64:320 + g * 64], vt.s(g)[:, g, s:s + 1], Sst.s(g)[:, G(g)],
                              ALU.mult, ALU.add)
                    for g in range(4):
                        k.stt(junk2.s(g)[:, G(g)], Sst.s(g)[:, G(g)], 1.0, P2[:, G(g)], ALU.mult, ALU.mult,
                              accum=Y.s(g)[:, g, col:col + 1])

            _dbg = os.environ.get("KDBG_A", "")
            prep(0)
            for b in range(nb):
                if b + 1 < nb:
                    prep(b + 1)
                if _dbg != "prep":
                    scan_block(b)
            if _dbg in ("prep", "scan"):
                A.release(m0)
                return
            if not latent and _dbg != "nonsr":
                for dd in range(2):
                    k.dma(DV("nsr", d["nsr"][job["j"], l, dd].rearrange("h i j -> i h j")),
                          RV(Sst[dd * 64:(dd + 1) * 64, :], "p (h j) -> p h j", j=64))
            A.release(mscan)
            WG = A.b(8 * 256, "p (c n) -> p c n", n=256)
            load_w(WG, l, XA_G, 256)
            t2 = A.f(256)
            Yr = A.f(512, "p (g u) -> p g u", u=128)
            E2 = A.f(64)
            k.tt("dve", E2[:], IDF[:, 0:64], IDF[:, 64:128], ALU.add)
            yt = A.f(256)
            yc = A.f(256)
            rt = A.f(256)
            gs = A.f(24)
            gt = A.f(256)
            g3 = lambda v: RV(v, "p (g j) -> p g j", j=64)
            for i in range(nt):
                hi = T - 1 - 128 * i
                lo = hi - 128
                src = Y[:, :, hi:lo:-1] if lo >= 0 else Y[:, :, hi::-1]
                k.cp("dve", Yr[:, :, :], src)
                k.cp("act", Yr[0:64, :, :], Y[0:64, :, i * 128:(i + 1) * 128])
                py = nps()
                for g in range(4):
                    k.mm(py[:, G(g)], Yr[:, g, :], E2[:, :])
                k.cp("act", yt[:], py[:, 0:256])
                k.red(gs[:, 0:4], g3(yt[:]))
                k.ts("dve", gs[:, 4:8], gs[:, 0:4], -1.0 / 64, None, ALU.mult)
                k.tt("dve", g3(yc[:]), g3(yt[:]), BC(gs[:, 4:8], 64), ALU.add)
                k.tt("dve", t2[:], yc[:], yc[:], ALU.mult)
                k.red(gs[:, 8:12], g3(t2[:]))
                rsqrt_col(k, gs[:, 12:16], gs[:, 8:12], 1.0 / 64, GN_EPS, gs[:, 16:20])
                k.tt("dve", g3(yc[:]), g3(yc[:]), BC(gs[:, 12:16], 64), ALU.mult)
                k.tt("dve", yc[:], yc[:], ROWS[:, R_GNG:R_GNG + 256], ALU.mult)
                k.tt("dve", yc[:], yc[:], ROWS[:, R_GNB:R_GNB + 256], ALU.add)
                pr, pr2 = nps(), nps()
                tok = slice(i * 128, (i + 1) * 128)
                for c in range(8):
                    k.mm(pr[:, 0:512], HT[:, c, tok], WA[:, c, 0:512], c == 0, c == 7)
                for c in range(8):
                    k.mm(pr2[:, 0:256], HT[:, c, tok], WA[:, c, 512:768], c == 0, c == 7)
                k.cp("act", rt[:], pr[:, 0:256])
                k.tt("dve", t2[:], rt[:], pr[:, 256:512], ALU.mult)
                k.tt("dve", t2[:], t2[:], ROWS[:, R_RK:R_RK + 256], ALU.mult)
                k.red(gs[:, 20:24], g3(t2[:]))
                k.tt("dve", g3(t2[:]), g3(pr2[:, 0:256]), BC(gs[:, 20:24], 64), ALU.mult)
                k.tt("dve", yc[:], yc[:], t2[:], ALU.add)
                pt, pg = nps(), nps()
                for ct in range(2):
                    k.mm(pt[:, ct * 128:(ct + 1) * 128], yc[:, ct * 128:(ct + 1) * 128], IDF[:, :])
                    for c in range(8):
                        k.mm(pg[:, ct * 128:(ct + 1) * 128], WG[:, c, ct * 128:(ct + 1) * 128], HT[:, c, tok], c == 0, c == 7)
                k.act(gt[:], pg[:, 0:256], AF.Silu)
                k.tt("dve", YCT[:, 0:2, tok], RV(pt[:, 0:256], "p (c u) -> p c u", u=128),
                     RV(gt[:], "p (c u) -> p c u", u=128), ALU.mult)
            A.release(m0)


        def phase_a2(job, l):
            Tt, latent = job["T"], job["latent"]
            BT = 128
            NQB = BT // 64
            m0 = A.mark()
            WA = A.b(8 * 896, "p (c n) -> p c n", n=896)
            load_w(WA, l, XA_TM, 896)
            WUP = A.b(512)
            m1 = A.mark()
            wst = A.f(512)
            k.dma(wst[:], DV("wup", d["wup"][l]))
            k.cp("pool", WUP[:], wst[:])
            A.release(m1)
            CA = A.f(30)
            k.dma(CA[:], DV("colA", d["colA"][l]))
            CM = A.f(640)
            k.dma(CM[:], DV("cmask", d["cmask"]))
            MS, MI, MLt, BO = CM[:, 0:128], CM[:, 128:256], CM[:, 256:384], CM[:, 384:512]
            ZER = CM[:, 576:640]
            Y2 = A.f(4 * Tt, "p (h t) -> p h t", t=Tt)
            mstream = A.mark()

            def mkbuf(dd):
                B = {}
                for nm in ("rT", "kT", "sg", "av", "kk", "t1", "t2", "Wc", "Wi"):
                    B[nm] = A.f(BT)
                B["LT"] = A.b(BT)
                B["hb"] = A.b(8 * BT, "p (c u) -> p c u", u=BT) if dd == 1 else None
                B["KR"] = A.f(NQB * 256, "p (q n) -> p q n", n=256)
                B["NK"] = A.f(NQB * 256, "p (q n) -> p q n", n=256)
                B["C"] = []
                for _ in range(NQB):
                    Cq = dict(AP=A.f(256), BQ=A.f(256), NKt=A.f(256), W=[A.f(384), A.f(384)])
                    Cq["At"] = TileH(Cq["W"][1].name, Cq["W"][1].ap[:, 256:384]) if not os.environ.get("KNOALIAS") else A.f(128)
                    B["C"].append(Cq)
                B["Vt"] = A.f(NQB * 64, "p (q n) -> p q n", n=64)
                B["S"] = A.f(64)
                B["G"] = A.f(64)
                B["U"] = A.f(64)
                B["So"] = A.f(128) if not latent else None
                return B

            def stream(dd, ct, B, sq0, T, jq):
                ps = nps
                nblk = T // BT
                S0 = B["S"]
                if latent:
                    k.dma(S0[:], DV("srwT", d["srwT"][l, dd, 2 * ct:2 * ct + 2].rearrange("h j i -> (h j) i")))
                else:
                    k.memset("dve", S0[:], 0.0)
                k.memset("dve", B["KR"][:], 0.0)
                k.memset("dve", B["NK"][:], 0.0)
                cw0 = CA[:, ct * 6 + dd:ct * 6 + dd + 1]
                ca0 = CA[:, ct * 6 + 2 + dd:ct * 6 + 3 + dd]
                ckk = CA[:, ct * 6 + 4:ct * 6 + 5]
                cka = CA[:, ct * 6 + 5:ct * 6 + 6]
                for blk in range(nblk):
                    b0 = blk * BT
                    if dd == 0:
                        hsrc = lambda c, u0, n: HT[:, c, sq0 + b0 + u0:sq0 + b0 + u0 + n]
                    else:
                        hi = sq0 + T - 1 - b0
                        lo = hi - BT
                        src = HT[:, :, hi:lo:-1] if lo >= 0 else HT[:, :, hi::-1]
                        k.cp("dve", B["hb"][:], src)
                        hsrc = lambda c, u0, n: B["hb"][:, c, u0:u0 + n]
                    p1, p2 = ps(), ps()
                    for c in range(8):
                        k.mm(p1[:, 0:BT], WA[:, c, ct * 128:(ct + 1) * 128], hsrc(c, 0, BT), c == 0, c == 7)
                    for c in range(8):
                        k.mm(p1[:, BT:2 * BT], WA[:, c, 256 + ct * 128:256 + (ct + 1) * 128], hsrc(c, 0, BT), c == 0, c == 7)
                    for c in range(8):
                        k.mm(p2[:, 0:BT], WA[:, c, 768:896], hsrc(c, 0, BT), c == 0, c == 7)
                    k.cp("act", B["rT"][:], p1[:, 0:BT])
                    k.cp("act", B["kT"][:], p1[:, BT:2 * BT])
                    k.act(B["LT"][0:64, :], p2[0:64, 0:BT], AF.Tanh)
                    k.cp("act", B["LT"][64:128, :], p2[64:128, 0:BT])
                    yield
                    wc = dd * 256 + ct * 128
                    p3, p5 = ps(), ps()
                    k.mm(p3[:, 0:BT], WUP[0:64, wc:wc + 128], B["LT"][0:64, :])
                    k.mm(p5[:, 0:BT], WUP[64:128, wc:wc + 128], B["LT"][64:128, :])
                    k.act(B["sg"][:], p3[:, 0:BT], AF.Sigmoid, bias=cw0)
                    k.act(B["av"][:], p5[:, 0:BT], AF.Sigmoid, bias=ca0)
                    k.act(B["sg"][:], B["sg"][:], AF.Exp, scale=-math.exp(-0.5))
                    yield
                    k.ts("dve", B["kk"][:], B["kT"][:], ckk, None, ALU.mult)
                    k.tt("dve", B["t1"][:], B["kk"][:], B["kk"][:], ALU.mult)
                    yield
                    p4 = ps()
                    k.mm(p4[:, 0:BT], BO, B["t1"][:])
                    k.ts("dve", B["t1"][:], p4[:, 0:BT], 1e-12, None, ALU.add)
                    k.act(B["t1"][:], B["t1"][:], AF.Sqrt)
                    k.recip(B["t2"][:], B["t1"][:])
                    k.tt("dve", B["kk"][:], B["kk"][:], B["t2"][:], ALU.mult)
                    yield
                    q3 = lambda v: RV(v, "p (q s) -> p q s", s=64)
                    for q in range(NQB):
                        k.scan(B["Wc"][:, q * 64:(q + 1) * 64], B["sg"][:, q * 64:(q + 1) * 64], ZER, 1.0)
                    k.recip(B["Wi"][:], B["Wc"][:])
                    k.stt(B["t1"][:], B["kk"][:], -1.0, B["av"][:], ALU.mult, ALU.mult)
                    k.ts("dve", B["t2"][:], B["av"][:], -1.0, cka, ALU.add, ALU.mult)
                    k.stt(B["t2"][:], B["t2"][:], 1.0, B["kT"][:], ALU.add, ALU.mult)
                    for hh in range(2):
                        hp = slice(hh * 64, (hh + 1) * 64)
                        cs = slice(hh * 64, (hh + 1) * 64)
                        KRv = B["KR"]
                        NKv = B["NK"]
                        k.tt("dve", KRv[hp, :, hh * 64 + 1:hh * 64 + 64], q3(B["kk"][hp, :])[:, :, 1:64] if False else
                             V(B["kk"].ap[hp, :].rearrange("p (q s) -> p q s", s=64)[:, :, 1:64], B["kk"][:].key),
                             V(B["Wc"].ap[hp, :].rearrange("p (q s) -> p q s", s=64)[:, :, 0:63], B["Wc"][:].key), ALU.mult)
                        k.cp("dve", KRv[hp, :, hh * 64:hh * 64 + 1],
                             V(B["kk"].ap[hp, :].rearrange("p (q s) -> p q s", s=64)[:, :, 0:1], B["kk"][:].key))
                        k.tt("dve", KRv[hp, :, 128 + hh * 64:128 + hh * 64 + 64], q3(B["rT"][hp, :]), q3(B["Wc"][hp, :]), ALU.mult)
                        k.tt("dve", NKv[hp, :, hh * 64:hh * 64 + 64], q3(B["t1"][hp, :]), q3(B["Wi"][hp, :]), ALU.mult)
                        k.tt("dve", NKv[hp, :, 128 + hh * 64:128 + hh * 64 + 64], q3(B["t2"][hp, :]), q3(B["Wi"][hp, :]), ALU.mult)
                    yield
                    KR, NK = B["KR"], B["NK"]
                    pcs = []
                    for q in range(NQB):
                        C = B["C"][q]
                        for hh in range(2):
                            pv = ps()
                            vc = 512 + (2 * ct + hh) * 64
                            for c in range(8):
                                k.mm(pv[hh * 64:(hh + 1) * 64, 0:64], hsrc(c, q * 64, 64), WA[:, c, vc:vc + 64], c == 0, c == 7)
                            k.cp("act", B["Vt"][hh * 64:(hh + 1) * 64, q, :], pv[hh * 64:(hh + 1) * 64, 0:64])
                        pa, pb, pc = ps(), ps(), ps()
                        k.mm(pa[:, 0:256], NK[:, q, 0:128], KR[:, q, :])
                        k.mm(pb[:, 0:256], NK[:, q, 128:256], KR[:, q, :])
                        k.mm(pc[:, 0:128], KR[:, q, 0:128], NK[:, q, 0:128])
                        k.mm(pc[:, 128:256], NK[:, q, 0:128], IDF[:, :])
                        k.mm(pc[:, 256:384], NK[:, q, 128:256], IDF[:, :])
                        k.tt("dve", C["AP"][:, 0:128], pa[:, 0:128], MS, ALU.mult)
                        k.tt("dve", C["AP"][:, 128:256], pa[:, 128:256], MI, ALU.mult)
                        k.tt("dve", C["BQ"][:, 0:128], pb[:, 0:128], MS, ALU.mult)
                        k.tt("dve", C["BQ"][:, 128:256], pb[:, 128:256], MI, ALU.mult)
                        k.tt("dve", C["At"][:], pc[:, 0:128], MLt, ALU.mult)
                        k.cp("act", RV(C["NKt"][:], "p (a n) -> p a n", n=128), RV(pc[:, 128:384], "p (a n) -> p a n", n=128))
                        yield
                    ev = lambda v: V(v.ap.rearrange("p (a n) -> p a n", n=128)[:, 0:3:2, :], v.key)
                    for q in range(NQB):
                        C = B["C"][q]
                        W0 = C["W"][0]
                        px = ps()
                        k.mm(px[:, 0:128], C["At"][:], C["AP"][:, 0:128])
                        k.mm(px[:, 256:384], C["AP"][:, 0:128], C["At"][:])
                        k.tt("dve", W0[:, 128:256], C["AP"][:, 0:128], IDF[:, :], ALU.add)
                        k.cp("act", ev(W0[:, :]), ev(px[:, 0:384]))
                    yield
                    cur = 0
                    for lev in range(1, 6):
                        nxt = 1 - cur
                        last = lev == 5
                        for q in range(NQB):
                            C = B["C"][q]
                            Wc_, Wn = C["W"][cur], C["W"][nxt]
                            px = ps()
                            if last:
                                k.mm(px[:, 128:256], Wc_[:, 256:384], Wc_[:, 128:256])
                            else:
                                k.mm(px[:, 0:256], Wc_[:, 256:384], Wc_[:, 0:256])
                                k.mm(px[:, 256:384], Wc_[:, 0:128], Wc_[:, 256:384])
                                k.cp("act", ev(Wn[:, :]), ev(px[:, 0:384]))
                            k.tt("dve", Wn[:, 128:256], Wc_[:, 128:256], px[:, 128:256], ALU.add)
                        cur = nxt
                        yield
                    for q in range(NQB):
                        C = B["C"][q]
                        Tbd = C["W"][cur][:, 128:256]
                        pg = ps()
                        k.mm(pg[:, 0:64], KR[:, q, 0:128], S0[:], True, False)
                        k.mm(pg[:, 0:64], C["BQ"][:, 0:128], B["Vt"][:, q, :], False, True)
                        k.cp("act", B["G"][:], pg[:, 0:64])
                        yield
                        pu = ps()
                        k.mm(pu[:, 0:64], Tbd, B["G"][:])
                        k.cp("act", B["U"][:], pu[:, 0:64])
                        yield
                        psn = ps()
                        k.mm(psn[:, 0:64], C["NKt"][:, 0:128], B["U"][:], True, False)
                        k.mm(psn[:, 0:64], C["NKt"][:, 128:256], B["Vt"][:, q, :], False, False)
                        k.mm(psn[:, 0:64], IDF[:, :], S0[:], False, True)
                        po = psn[dd * 64:(dd + 1) * 64, 128:256]
                        k.mm(po, S0[:], KR[:, q, 128:256], True, False)
                        k.mm(po, B["U"][:], C["AP"][:, 128:256], False, False)
                        k.mm(po, B["Vt"][:, q, :], C["BQ"][:, 128:256], False, True)
                        k.act(S0[:], psn[:, 0:64], AF.Copy, scale=B["Wc"][:, q * 64 + 63:q * 64 + 64])
                        t0 = b0 + q * 64
                        if dd == 0:
                            dst = Y2[dd * 64:(dd + 1) * 64, 2 * ct:2 * ct + 2, sq0 + t0:sq0 + t0 + 64]
                        else:
                            hi2 = sq0 + T - 1 - t0
                            lo2 = hi2 - 64
                            dst = Y2[64:128, 2 * ct:2 * ct + 2, hi2:lo2:-1] if lo2 >= 0 else Y2[64:128, 2 * ct:2 * ct + 2, hi2::-1]
                        k.cp("dve", dst, RV(po, "p (h t) -> p h t", t=64))
                        yield
                if not latent:
                    pt = ps()
                    k.mm(pt[0:64, 0:128], S0[:], IDF[:, :])
                    k.cp("act", B["So"][0:64, :], pt[0:64, 0:128])
                    for hh in range(2):
                        k.dma(DV("nsr", d["nsr"][jq, l, dd, 2 * ct + hh]), B["So"][0:64, hh * 64:(hh + 1) * 64])

            bufs = [mkbuf(0), mkbuf(1), mkbuf(0), mkbuf(1)]
            for (sq0, Tq, jq) in job["seqs"]:
                gens = [stream(0, 0, bufs[0], sq0, Tq, jq), stream(1, 0, bufs[1], sq0, Tq, jq),
                        stream(0, 1, bufs[2], sq0, Tq, jq), stream(1, 1, bufs[3], sq0, Tq, jq)]
                alive = [True] * 4
                _stop = int(os.environ.get("KDBG_STOP", "0"))
                _n = 0
                while any(alive):
                    for gi, g in enumerate(gens):
                        if alive[gi] and _n >= gi * 3:
                            try:
                                next(g)
                            except StopIteration:
                                alive[gi] = False
                    _n += 1
                    if _stop and _n >= _stop:
                        break
            A.release(mstream)
            if os.environ.get("KDBG_STOP"):
                A.release(m0)
                return
            WG = A.b(8 * 256, "p (c n) -> p c n", n=256)
            load_w(WG, l, XA_G, 256)
            E2 = A.f(64)
            k.tt("dve", E2[:], IDF[:, 0:64], IDF[:, 64:128], ALU.add)
            T = Tt
            NP = min(512, T)
            ys = A.f(NP)
            yc = A.f(NP)
            sq = A.f(NP)
            rr = A.f(NP)
            vv = A.f(NP)
            gg = A.f(NP)
            for n0 in range(0, T, NP):
                for ct in range(2):
                    gcol = lambda q: CA[:, 24 + ct * 3 + q:24 + ct * 3 + q + 1]
                    cs = slice(ct * 128, (ct + 1) * 128)
                    p1 = nps()
                    for hh in range(2):
                        k.mm(p1[hh * 64:(hh + 1) * 64, 0:NP], E2[:], Y2[:, 2 * ct + hh, n0:n0 + NP])
                    k.cp("act", ys[:], p1[:, 0:NP])
                    p2 = nps()
                    k.mm(p2[:, 0:NP], BO, ys[:])
                    k.stt(yc[:], p2[:, 0:NP], -1.0 / 64, ys[:], ALU.mult, ALU.add)
                    k.tt("dve", sq[:], yc[:], yc[:], ALU.mult)
                    p3 = nps()
                    k.mm(p3[:, 0:NP], BO, sq[:])
                    k.ts("dve", sq[:], p3[:, 0:NP], 1.0 / 64, GN_EPS, ALU.mult, ALU.add)
                    k.act(sq[:], sq[:], AF.Sqrt)
                    k.recip(sq[:], sq[:])
                    k.tt("dve", yc[:], yc[:], sq[:], ALU.mult)
                    k.ts("dve", yc[:], yc[:], gcol(0), gcol(1), ALU.mult, ALU.add)
                    pr, pk, pv2, pg2 = nps(), nps(), nps(), nps()
                    for c in range(8):
                        k.mm(pr[:, 0:NP], WA[:, c, ct * 128:(ct + 1) * 128], HT[:, c, n0:n0 + NP], c == 0, c == 7)
                    for c in range(8):
                        k.mm(pk[:, 0:NP], WA[:, c, 256 + ct * 128:256 + (ct + 1) * 128], HT[:, c, n0:n0 + NP], c == 0, c == 7)
                    for c in range(8):
                        k.mm(pv2[:, 0:NP], WA[:, c, 512 + ct * 128:512 + (ct + 1) * 128], HT[:, c, n0:n0 + NP], c == 0, c == 7)
                    for c in range(8):
                        k.mm(pg2[:, 0:NP], WG[:, c, cs], HT[:, c, n0:n0 + NP], c == 0, c == 7)
                    k.cp("act", rr[:], pr[:, 0:NP])
                    k.stt(rr[:], rr[:], gcol(2), pk[:, 0:NP], ALU.mult, ALU.mult)
                    k.cp("act", vv[:], pv2[:, 0:NP])
                    k.act(gg[:], pg2[:, 0:NP], AF.Silu)
                    p4 = nps()
                    k.mm(p4[:, 0:NP], BO, rr[:])
                    k.tt("dve", vv[:], vv[:], p4[:, 0:NP], ALU.mult)
                    k.tt("dve", yc[:], yc[:], vv[:], ALU.add)
                    k.tt("dve", YCT[:, ct, n0:n0 + NP], yc[:], gg[:], ALU.mult)
            A.release(m0)

        def phase_c(job, l):
            T, latent = job["T"], job["latent"]
            m0 = A.mark()
            WC = A.b(8 * 512, "p (c n) -> p c n", n=512)
            load_w(WC, l, XC_X, 512)
            BD = A.b(1024, "p (m q) -> p m q", q=128)
            m1 = A.mark()
            bst = A.f(1024, "p (m q) -> p m q", q=128)
            k.dma(bst[:], DV("bd", d["bd"][l].rearrange("m p q -> p m q")))
            k.cp("pool", BD[:], bst[:])
            A.release(m1)
            CP = A.f(NCOLP)
            k.dma(CP[:], DV("colp", d["colp"][l]))
            c8 = A.f(4)
            cc = A.f(4)
            for ct in range(2):
                k.act(cc[:, ct * 2:ct * 2 + 2], CP[:, ct * 11 + 9:ct * 11 + 11], AF.Exp, scale=-1.0)
            k.act(cc[:], cc[:], AF.Ln, bias=1.0)
            k.ts("dve", c8[:], cc[:], -8.0, None, ALU.mult)
            H0 = A.f(4)
            if latent:
                k.dma(H0[:], DV("slrT", d["slrT"][l]))
            T = job["seqs"][0][1]
            NSs = [A.f(4) for _ in job["seqs"]]
            xT = A.f(T + 4)
            gT = A.f(T)
            xc = A.f(T)
            xcb = A.b(T)
            ga = A.f(T)
            gx = A.f(T)
            aT = A.f(T)
            u = A.f(T)
            hf = A.f(T)
            hbk = A.f(T)
            for (qi_, sq0, ct) in [(qi__, sq[0], ct_) for qi__, sq in enumerate(job["seqs"]) for ct_ in range(2)]:
                NS = NSs[qi_]
                k.memset("dve", xT[:, 0:2], 0.0)
                k.memset("dve", xT[:, T + 2:T + 4], 0.0)
                for tb in range(0, T, 512):
                    n = min(512, T - tb)
                    ps, ps2 = nps(), nps()
                    for c in range(8):
                        k.mm(ps[:, 0:n], WC[:, c, ct * 128:(ct + 1) * 128], HT[:, c, sq0 + tb:sq0 + tb + n], c == 0, c == 7)
                    k.cp("act", xT[:, 2 + tb:2 + tb + n], ps[:, 0:n])
                    for c in range(8):
                        k.mm(ps2[:, 0:n], WC[:, c, 256 + ct * 128:256 + (ct + 1) * 128], HT[:, c, sq0 + tb:sq0 + tb + n], c == 0, c == 7)
                    k.act(gT[:, tb:tb + n], ps2[:, 0:n], AF.Silu)
                cw = lambda q: CP[:, ct * 11 + q:ct * 11 + q + 1]
                k.ts("dve", xc[:], xT[:, 0:T], cw(0), cw(4), ALU.mult, ALU.add)
                for q in range(1, 4):
                    k.stt(xc[:], xT[:, q:q + T], cw(q), xc[:], ALU.mult, ALU.add)
                k.cp("act", xcb[:], xc[:])
                for dd in range(2):
                    for gate, gbuf in ((0, ga), (1, gx)):
                        for tb in range(0, T, 512):
                            n = min(512, T - tb)
                            ps = nps()
                            k.mm(ps[:, 0:n], BD[:, dd * 4 + gate * 2 + ct, :], xcb[:, tb:tb + n])
                            bcol = ct * 11 + (5 if gate == 0 else 7) + dd
                            k.act(gbuf[:, tb:tb + n], ps[:, 0:n], AF.Sigmoid, bias=CP[:, bcol:bcol + 1])
                    k.act(aT[:], ga[:], AF.Exp, scale=c8[:, ct * 2 + dd:ct * 2 + dd + 1])
                    k.tt("dve", u[:], aT[:], aT[:], ALU.mult)
                    k.act(u[:], u[:], AF.Sqrt, bias=1.0, scale=-1.0)
                    k.tt("dve", u[:], u[:], gx[:], ALU.mult)
                    k.tt("dve", u[:], u[:], xc[:], ALU.mult)
                    col = ct * 2 + dd
                    h0 = H0[:, col:col + 1] if latent else 0.0
                    if dd == 0:
                        k.scan(hf[:], aT[:], u[:], h0)
                        k.cp("act", NS[:, col:col + 1], hf[:, T - 1:T])
                    else:
                        k.scan(hbk[:, ::-1], aT[:, ::-1], u[:, ::-1], h0)
                        k.cp("act", NS[:, col:col + 1], hbk[:, 0:1])
                k.tt("dve", hf[:], hf[:], hbk[:], ALU.add)
                k.tt("dve", YCT[:, 4 + ct, sq0:sq0 + T], hf[:], gT[:], ALU.mult)
            if not latent:
                for qi_, sq in enumerate(job["seqs"]):
                    k.dma(DV("nsl", d["nsl"][sq[2], l]), NSs[qi_][:])
            A.release(m0)

        def gate_transpose(ybuf, WGt, goff, chunk0, T):
            nt = T // 128
            gt = A.f(256)
            for i in range(nt):
                tok = slice(i * 128, (i + 1) * 128)
                pt, pg = nps(), nps()
                for ct in range(2):
                    k.mm(pt[:, ct * 128:(ct + 1) * 128], ybuf[:, i, ct * 128:(ct + 1) * 128], IDF[:, :])
                    for c in range(8):
                        k.mm(pg[:, ct * 128:(ct + 1) * 128], WGt[:, c, goff + ct * 128:goff + (ct + 1) * 128], HT[:, c, tok],
                             c == 0, c == 7)
                k.act(gt[:], pg[:, 0:256], AF.Silu)
                k.tt("dve", YCT[:, chunk0:chunk0 + 2, tok], RV(pt[:, 0:256], "p (c u) -> p c u", u=128),
                     RV(gt[:], "p (c u) -> p c u", u=128), ALU.mult)

        def phase_b(job, l):
            T, latent = job["T"], job["latent"]
            nt = T // 128
            nkt = nt + (2 if latent else 0)
            m0 = A.mark()
            WB = A.b(8 * 1152, "p (c n) -> p c n", n=1152)
            load_w(WB, l, XB_Q, 1152)
            qT = A.b(2 * T, "p (g t) -> p g t", t=T)
            kT = A.b(T + 256)
            Va = A.b(nkt * 2 * 65, "p (t h e) -> p t h e", h=2, e=65)
            es_ = A.f(4)
            ybuf = A.f(nt * 256, "p (t n) -> p t n", n=256)
            k.act(es_[:], ROWS[:, R_SINK:R_SINK + 4], AF.Exp)
            k.memset("dve", Va[:, :, :, 64:65], 1.0)
            if latent:
                RB = A.f(2048, "p (a t) -> p a t", t=1024)
                k.dma(RB[:], DV("ropeB", d["ropeB"]))
                ta = A.f(512)
                tb_ = A.f(512)
            for t0 in range(0, T, 512):
                n = min(512, T - t0)
                srcs = [(qT[:, 0, t0:t0 + n], 0, 256), (qT[:, 1, t0:t0 + n], 128, 384), (kT[:, t0:t0 + n], 512, 640)]
                for dst, o1, o2 in srcs:
                    ps = nps()
                    for c in range(8):
                        k.mm(ps[:, 0:n], WB[:, c, o1:o1 + 128], HT[:, c, t0:t0 + n], c == 0, c == 7)
                    if latent:
                        ps2 = nps()
                        for c in range(8):
                            k.mm(ps2[:, 0:n], WB[:, c, o2:o2 + 128], HT[:, c, t0:t0 + n], c == 0, c == 7)
                        k.tt("dve", ta[:, 0:n], ps[:, 0:n], RB[:, 0, t0:t0 + n], ALU.mult)
                        k.tt("dve", tb_[:, 0:n], ps2[:, 0:n], RB[:, 1, t0:t0 + n], ALU.mult)
                        k.tt("dve", dst, ta[:, 0:n], tb_[:, 0:n], ALU.add)
                    else:
                        k.cp("act", dst, ps[:, 0:n])
            kv = [A.f(256), A.f(256)]
            for i in range(nt):
                tok = slice(i * 128, (i + 1) * 128)
                ps = nps()
                for c in range(8):
                    k.mm(ps[:, 0:128], HT[:, c, tok], WB[:, c, 512:640], c == 0, c == 7)
                for c in range(8):
                    k.mm(ps[:, 128:256], HT[:, c, tok], WB[:, c, 768:896], c == 0, c == 7)
                k.cp("act", Va[:, i, :, 0:64], RV(ps[:, 128:256], "p (h e) -> p h e", e=64))
                if not latent:
                    k.cp("act", kv[i % 2][:], ps[:, 0:256])
                    sq0, Tq, jq = [sq for sq in job["seqs"] if sq[0] <= i * 128 < sq[0] + sq[1]][0]
                    tl = slice(i * 128 - sq0, (i + 1) * 128 - sq0)
                    k.dma(DV("nwk", d["nwk"][jq, l, tl, :]), kv[i % 2][:, 0:128])
                    k.dma(DV("nwv", d["nwv"][jq, l, tl, :]), kv[i % 2][:, 128:256])
            if latent:
                for cj in range(2):
                    ck = kv[cj]
                    k.dma(ck[:, 0:128], DV("cwk", d["cwk"][l, cj * 128:(cj + 1) * 128, :]))
                    k.dma(ck[:, 128:256], DV("cwv", d["cwv"][l, cj * 128:(cj + 1) * 128, :]))
                    ps = nps()
                    k.mm(ps[:, 0:128], ck[:, 0:128], IDF[:, :])
                    k.cp("act", kT[:, T + cj * 128:T + (cj + 1) * 128], ps[:, 0:128])
                    k.cp("act", Va[:, nt + cj, :, 0:64], RV(ck[:, 128:256], "p (h e) -> p h e", e=64))
            ETo = A.b(nt * 384, "p (t n) -> p t n", n=384)
            ETc = A.b(2 * T, "p (t n) -> p t n", n=T) if latent else None
            osb = A.f(4)
            for kvh in range(2):
                hs = slice(kvh * 64, (kvh + 1) * 64)
                for g in range(2):
                    h = kvh * 2 + g
                    if latent:
                        for kb in range(nt):
                            qb0, qb1 = max(kb - 1, 0), min(kb + 1, nt - 1)
                            nq = (qb1 - qb0 + 1) * 128
                            s0 = (qb0 - (kb - 1)) * 128
                            ps = nps()
                            k.mm(ps[:, 0:nq], kT[hs, kb * 128:(kb + 1) * 128], qT[hs, g, qb0 * 128:qb0 * 128 + nq])
                            k.act(ETo[:, kb, s0:s0 + nq], ps[:, 0:nq], AF.Exp, scale=0.125)
                            if kb - 1 >= 0:
                                k.tt("pool", ETo[:, kb, 0:128], ETo[:, kb, 0:128], MSK[:, 0:128], ALU.mult)
                            if kb + 1 <= nt - 1:
                                k.tt("pool", ETo[:, kb, 256:384], ETo[:, kb, 256:384], MSK[:, 128:256], ALU.mult)
                        for cj in range(2):
                            for t0 in range(0, T, 512):
                                ps = nps()
                                k.mm(ps[:, 0:512], kT[hs, T + cj * 128:T + (cj + 1) * 128], qT[hs, g, t0:t0 + 512])
                                k.act(ETc[:, cj, t0:t0 + 512], ps[:, 0:512], AF.Exp, scale=0.125)
                    else:
                        for (sq0, Tq, jq) in job["seqs"]:
                            for kb in range(sq0 // 128, (sq0 + Tq) // 128):
                                ps = nps()
                                k.mm(ps[:, 0:Tq], kT[hs, kb * 128:(kb + 1) * 128], qT[hs, g, sq0:sq0 + Tq])
                                k.act(ETo[:, kb, 0:Tq], ps[:, 0:Tq], AF.Exp, scale=0.125)
                    for qi in range(nt):
                        terms = []
                        if latent:
                            for kb in (qi - 1, qi, qi + 1):
                                if 0 <= kb < nt:
                                    sl = (qi - kb + 1) * 128
                                    terms.append((ETo[:, kb, sl:sl + 128], Va[:, kb, kvh, :]))
                            for cj in range(2):
                                terms.append((ETc[:, cj, qi * 128:(qi + 1) * 128], Va[:, nt + cj, kvh, :]))
                        else:
                            sq0, Tq, jq = [sq for sq in job["seqs"] if sq[0] <= qi * 128 < sq[0] + sq[1]][0]
                            ql = qi * 128 - sq0
                            for kb in range(sq0 // 128, (sq0 + Tq) // 128):
                                terms.append((ETo[:, kb, ql:ql + 128], Va[:, kb, kvh, :]))
                        po = nps()
                        for ti, (lt, rv) in enumerate(terms):
                            k.mm(po[:, 0:65], lt, rv, ti == 0, ti == len(terms) - 1)
                        k.tt("dve", osb[:, 0:1], po[:, 64:65], es_[:, h:h + 1], ALU.add)
                        k.recip(osb[:, 1:2], osb[:, 0:1])
                        k.ts("dve", ybuf[:, qi, h * 64:(h + 1) * 64], po[:, 0:64], osb[:, 1:2], None, ALU.mult)
            gate_transpose(ybuf, WB, 896, 2, T)
            A.release(m0)

        def phase_d(job, l):
            T, latent = job["T"], job["latent"]
            nt = T // 128
            nkt = nt + (2 if latent else 0)
            TK = nkt * 128
            lam_init = 0.8 - 0.6 * math.exp(-0.3 * l)
            m0 = A.mark()
            WDq = A.b(8 * 1024, "p (c n) -> p c n", n=1024)
            load_w(WDq, l, XD_Q, 1024)
            WDv = A.b(8 * 512, "p (c n) -> p c n", n=512)
            load_w(WDv, l, XD_V, 512)
            qT = A.b(4 * T, "p (h t) -> p h t", t=T)
            kT = A.b(4 * TK, "p (h t) -> p h t", t=TK)
            Va = A.b(nkt * 4 * 65, "p (t h e) -> p t h e", h=4, e=65)
            ybuf = A.f(nt * 256, "p (t n) -> p t n", n=256)
            k.memset("dve", Va[:, :, :, 64:65], 1.0)
            lm = A.f(64)
            lc = A.f(8)
            k.tt("dve", lm[:, 0:32], ROWS[:, R_DLAM:R_DLAM + 32], ROWS[:, R_DLAM + 32:R_DLAM + 64], ALU.mult)
            k.tt("dve", lm[:, 32:64], ROWS[:, R_DLAM + 64:R_DLAM + 96], ROWS[:, R_DLAM + 96:R_DLAM + 128], ALU.mult)
            k.red(lc[:, 0:2], RV(lm[:], "p (a j) -> p a j", j=32))
            k.act(lc[:, 2:4], lc[:, 0:2], AF.Exp)
            k.tt("dve", lc[:, 4:5], lc[:, 2:3], lc[:, 3:4], ALU.subtract)
            k.ts("dve", lc[:, 5:6], lc[:, 4:5], lam_init, -1.0, ALU.add, ALU.mult)
            mrope = A.mark()
            if latent:
                RD = A.f(2048, "p (a t) -> p a t", t=1024)
                k.dma(RD[:], DV("ropeD", d["ropeD"]))
                ta = A.f(512)
                tb_ = A.f(512)
            for t0 in range(0, T, 512):
                n = min(512, T - t0)
                for h in range(4):
                    for dst, o1, o2 in ((qT[0:64, h, t0:t0 + n], h * 64, 256 + h * 64),
                                        (kT[0:64, h, t0:t0 + n], 512 + h * 64, 768 + h * 64)):
                        ps = nps()
                        for c in range(8):
                            k.mm(ps[0:64, 0:n], WDq[:, c, o1:o1 + 64], HT[:, c, t0:t0 + n], c == 0, c == 7)
                        if latent:
                            ps2 = nps()
                            for c in range(8):
                                k.mm(ps2[0:64, 0:n], WDq[:, c, o2:o2 + 64], HT[:, c, t0:t0 + n], c == 0, c == 7)
                            k.tt("dve", ta[0:64, 0:n], ps[0:64, 0:n], RD[0:64, 0, t0:t0 + n], ALU.mult)
                            k.tt("dve", tb_[0:64, 0:n], ps2[0:64, 0:n], RD[0:64, 1, t0:t0 + n], ALU.mult)
                            k.tt("dve", dst, ta[0:64, 0:n], tb_[0:64, 0:n], ALU.add)
                        else:
                            k.cp("act", dst, ps[0:64, 0:n])
            A.release(mrope)
            kv = [A.f(512), A.f(512)]
            for i in range(nt):
                tok = slice(i * 128, (i + 1) * 128)
                ps = nps()
                for c in range(8):
                    k.mm(ps[:, 0:256], HT[:, c, tok], WDq[:, c, 512:768], c == 0, c == 7)
                for c in range(8):
                    k.mm(ps[:, 256:512], HT[:, c, tok], WDv[:, c, 0:256], c == 0, c == 7)
                k.cp("act", Va[:, i, :, 0:64], RV(ps[:, 256:512], "p (h e) -> p h e", e=64))
                if not latent:
                    k.cp("act", kv[i % 2][:], ps[:, :])
                    sq0, Tq, jq = [sq for sq in job["seqs"] if sq[0] <= i * 128 < sq[0] + sq[1]][0]
                    tl = slice(i * 128 - sq0, (i + 1) * 128 - sq0)
                    k.dma(DV("ndk", d["ndk"][jq, l, tl, :]), kv[i % 2][:, 0:256])
                    k.dma(DV("ndv", d["ndv"][jq, l, tl, :]), kv[i % 2][:, 256:512])
            if latent:
                for cj in range(2):
                    ck = kv[cj]
                    k.dma(ck[:, 0:256], DV("cdk", d["cdk"][l, cj * 128:(cj + 1) * 128, :]))
                    k.dma(ck[:, 256:512], DV("cdv", d["cdv"][l, cj * 128:(cj + 1) * 128, :]))
                    for h in range(4):
                        ps = nps()
                        k.mm(ps[0:64, 0:128], ck[:, h * 64:(h + 1) * 64], IDF[:, :])
                        k.cp("act", kT[0:64, h, T + cj * 128:T + (cj + 1) * 128], ps[0:64, 0:128])
                    k.cp("act", Va[:, nt + cj, :, 0:64], RV(ck[:, 256:512], "p (h e) -> p h e", e=64))
            QC = min(512, T) if latent else 256
            ETs = [A.b(nkt * QC, "p (t n) -> p t n", n=QC) for _ in range(2)]
            o1s = A.f(64)
            dn = A.f(4)
            sc_ = DQ_SCALE
            for h in range(4):
                for q0 in range(0, T, QC):
                    nqt = QC // 128
                    acc = A_yacc
                    if latent:
                        klist = list(range(nkt))
                    else:
                        sq0, Tq, jq = [sq for sq in job["seqs"] if sq[0] <= q0 < sq[0] + sq[1]][0]
                        klist = list(range(sq0 // 128, (sq0 + Tq) // 128))
                    for m in range(2):
                        ms = slice(32 * m, 32 * m + 32)
                        ET = ETs[m]
                        for ki, kb in enumerate(klist):
                            ps = nps()
                            k.mm(ps[:, 0:QC], kT[ms, h, kb * 128:(kb + 1) * 128], qT[ms, h, q0:q0 + QC])
                            k.act(ET[:, ki, :], ps[:, 0:QC], AF.Exp, scale=sc_)
                    for m in range(2):
                        ET = ETs[m]
                        for qi in range(nqt):
                            po = nps()
                            for ki, kb in enumerate(klist):
                                k.mm(po[:, 0:65], ET[:, ki, qi * 128:(qi + 1) * 128], Va[:, kb, h, :], ki == 0, ki == len(klist) - 1)
                            ti = (q0 // 128) + qi
                            ysl = ybuf[:, ti, h * 64:(h + 1) * 64]
                            k.recip(dn[:, m:m + 1], po[:, 64:65])
                            if m == 0:
                                k.ts("dve", acc[:, qi * 64:(qi + 1) * 64], po[:, 0:64], dn[:, 0:1], None, ALU.mult)
                            else:
                                k.tt("dve", dn[:, 2:3], dn[:, 1:2], lc[:, 5:6], ALU.mult)
                                k.stt(ysl, po[:, 0:64], dn[:, 2:3], acc[:, qi * 64:(qi + 1) * 64], ALU.mult, ALU.add)
            sq = A.f(256)
            st = A.f(12)
            g3 = lambda v: RV(v, "p (g j) -> p g j", j=64)
            for i in range(nt):
                yi = ybuf[:, i, :]
                k.tt("dve", sq[:], yi, yi, ALU.mult)
                k.red(st[:, 0:4], g3(sq[:]))
                rsqrt_col(k, st[:, 4:8], st[:, 0:4], 1.0 / 64, 1e-6, st[:, 8:12])
                k.tt("dve", g3(yi), g3(yi), BC(st[:, 4:8], 64), ALU.mult)
                k.stt(yi, yi, 1.0 - lam_init, ROWS[:, R_SUB:R_SUB + 256], ALU.mult, ALU.mult)
            gate_transpose(ybuf, WDv, 256, 6, T)
            A.release(m0)

        def phase_o(job, l):
            T = job["T"]
            nt = T // 128
            m0 = A.mark()
            WO = A.b(8 * 1024, "p (c n) -> p c n", n=1024)
            m1 = A.mark()
            stg = [A.f(4096, "p (c n) -> p c n", n=512) for _ in range(2)]
            src = d["w_out"][l].rearrange("(c p) n -> p c n", p=128)
            for bi in range(2):
                k.dmaw(stg[bi % 2][:], DV("w_out", src[:, :, bi * 512:(bi + 1) * 512]))
                k.cp(("dve", "act")[bi % 2], WO[:, :, bi * 512:(bi + 1) * 512], stg[bi % 2][:])
            A.release(m1)
            yo = A.f(1024)
            junk = A.f(1024)
            xb = [A.f(1024), A.f(1024)]
            st = A.f(4 * nt)
            for i in range(nt):
                tok = slice(i * 128, (i + 1) * 128)
                k.dma(xb[i % 2][:], xsrc(job, l, i))
                pa, pb = nps(), nps()
                for c in range(8):
                    k.mm(pa[:, :], YCT[:, c, tok], WO[:, c, 0:512], c == 0, c == 7)
                for c in range(8):
                    k.mm(pb[:, :], YCT[:, c, tok], WO[:, c, 512:1024], c == 0, c == 7)
                k.cp("act", yo[:, 0:512], pa[:, :])
                k.cp("act", yo[:, 512:1024], pb[:, :])
                k.stt(junk[:], yo[:], 1.0, yo[:], ALU.mult, ALU.mult, accum=st[:, 4 * i:4 * i + 1])
                rsqrt_col(k, st[:, 4 * i + 1:4 * i + 2], st[:, 4 * i:4 * i + 1], 1.0 / 1024, 1e-6, st[:, 4 * i + 2:4 * i + 3])
                k.stt(junk[:], yo[:], st[:, 4 * i + 1:4 * i + 2], MG2[:], ALU.mult, ALU.mult)
                k.tt("dve", xb[i % 2][:], xb[i % 2][:], junk[:], ALU.add)
                k.dma(xdst(job, i), xb[i % 2][:])
            A.release(m0)

        DQ_SCALE = 32 ** -0.5
        A_yacc = None
        xpf = d["xp"].rearrange("a t n -> (a t) n")
        ypf = d["yp"].rearrange("a t n -> (a t) n")
        jobs = [
            dict(id=0, T=256, latent=False, seqs=[(0, 256, 0)], xin=d["xp"][0], yout=d["yp"][0], cv=0),
            dict(id=1, T=512, latent=False, seqs=[(0, 256, 0), (256, 256, 1)], xin=xpf, yout=ypf, cv=0),
            dict(id=2, T=1024, latent=True, seqs=[(0, 1024, 0)], xin=d["xs"], yout=d["ys"], cv=1),
        ]
        for ji in jobs_sel:
            job = jobs[ji]
            T = job["T"]
            nt = T // 128
            for l in range(nlayers):
                k.dma(ROWS[:], DV("rows", d["rows"][l:l + 1, :].to_broadcast([128, NR])))
                if "M" in phases:
                    phase_mod(job["cv"], l)
                if "H" in phases:
                    phase_h(job, l)
                if "A" in phases:
                    phase_a2(job, l)
                if "C" in phases:
                    phase_c(job, l)
                if "B" in phases:
                    phase_b(job, l)
                if "D" in phases:
                    A_yacc_m = A.mark()
                    A_yacc = A.f(256)
                    phase_d(job, l)
                    A.release(A_yacc_m)
                if "O" in phases:
                    phase_o(job, l)
        S.finish(block)
    return nc, A.peak


def _partner(d):
    h = d // 2
    q = h // 2
    idx = np.arange(d)
    p = np.empty(d, np.int64)
    for base in (0, h):
        p[base:base + q] = idx[base + q:base + h]
        p[base + q:base + h] = idx[base:base + q]
    return p


def _rope_tables(d, reps):
    h = d // 2
    q = h // 2
    n = 1024
    row = (np.arange(n) // 64).astype(np.float32)
    col = (np.arange(n) % 64).astype(np.float32)
    inv = (10000.0 ** (-np.arange(0, h, 2, dtype=np.float32) / h)).astype(np.float32)
    cos = np.zeros((d, n), np.float32)
    sin = np.zeros((d, n), np.float32)
    for base, pos in ((0, row), (h, col)):
        ang = (pos[None, :] * inv[:, None]).astype(np.float32)
        c, s_ = np.cos(ang).astype(np.float32), np.sin(ang).astype(np.float32)
        cos[base:base + q] = c
        cos[base + q:base + h] = c
        sin[base:base + q] = -s_
        sin[base + q:base + h] = s_
    return np.tile(cos, (reps, 1)), np.tile(sin, (reps, 1))


def _colidx():
    o = dict(ar=0, ak=256, av=512, awd=768, aad=832, ag=896, bq=1152, bk=1408, bv=1536, bg=1664, cx=1920, cg=2176,
             dq=2432, dk=2688, dv=2944, dg=3200)
    r = lambda a, n: np.arange(a, a + n)
    p64, p32 = _partner(64), _partner(32)
    bq = np.concatenate([o["bq"] + hh * 64 + np.arange(64) for hh in (0, 2, 1, 3)])
    bqs = np.concatenate([o["bq"] + hh * 64 + p64 for hh in (0, 2, 1, 3)])
    bk = r(o["bk"], 128)
    bks = np.concatenate([o["bk"] + hh * 64 + p64 for hh in range(2)])
    dq = r(o["dq"], 256)
    dqs = np.concatenate([o["dq"] + bb * 32 + p32 for bb in range(8)])
    dk = r(o["dk"], 256)
    dks = np.concatenate([o["dk"] + bb * 32 + p32 for bb in range(8)])
    idx = np.concatenate([r(0, 1152), bq, bqs, bk, bks, r(o["bv"], 128), r(o["bg"], 256), r(o["cx"], 256), r(o["cg"], 256),
                          dq, dqs, dk, dks, r(o["dv"], 256), r(o["dg"], 256)])
    assert idx.shape[0] == NX
    return idx


def _consts():
    identf = np.eye(128, dtype=np.float32)
    identb = identf.astype(ml_dtypes.bfloat16)
    sel = np.zeros((128, 64, 128), np.float32)
    for s_ in range(64):
        sel[s_, s_, 0:64] = 1.0
        sel[64 + s_, s_, 64:128] = 1.0
    sel = sel.reshape(128, 8192).astype(ml_dtypes.bfloat16)
    kk, qq = np.meshgrid(np.arange(128), np.arange(128), indexing="ij")
    masks = np.concatenate([(kk <= qq), (kk >= qq)], 1).astype(np.float32).astype(ml_dtypes.bfloat16)
    cb, sb_ = _rope_tables(64, 2)
    cd, sd = _rope_tables(32, 4)
    ropeB = np.ascontiguousarray(np.stack([cb, sb_], 1))
    ropeD = np.ascontiguousarray(np.stack([cd, sd], 1))
    p = np.arange(128)
    hh, ss = p // 64, p % 64
    same = (hh[:, None] == hh[None, :])
    cm = np.zeros((128, 640), np.float32)
    cm[:, 0:128] = same & (ss[:, None] < ss[None, :])
    cm[:, 128:256] = same & (ss[:, None] <= ss[None, :])
    cm[:, 256:384] = same & (ss[None, :] < ss[:, None])
    cm[:, 384:512] = same
    cm[:, 512:576] = 1.0 / 64
    return dict(identf=identf, identb=identb, masks=masks, ropeB=ropeB, ropeD=ropeD, cmask=cm)


def _prep_shared(inp):
    f = lambda a: np.ascontiguousarray(np.asarray(a, dtype=np.float32))
    sh = {}
    sh["w_mod"] = f(inp["w_mod"])
    sh["b_mod"] = f(inp["b_mod"])
    sh["g_pre"] = f(inp["g_pre"])
    sh["g_post"] = f(inp["g_post"])
    sh["w_in_x"] = f(np.asarray(inp["w_in"])[:, :, _colidx()])
    sh["w_out"] = f(inp["w_out"])
    rows = np.zeros((L, NR), np.float32)
    for l in range(L):
        rows[l, R_SUB:R_SUB + 256] = np.tile(np.asarray(inp["diff_subln_g"][l]), 4)
        rows[l, R_DLAM:R_DLAM + 128] = np.asarray(inp["diff_lambda"][l]).reshape(128)
        rows[l, R_SINK:R_SINK + 4] = inp["win_sink"][l]
    sh["rows"] = rows
    wup = np.zeros((L, 128, 512), np.float32)
    colp = np.zeros((L, 128, NCOLP), np.float32)
    bd = np.zeros((L, 8, 128, 128), np.float32)
    for l in range(L):
        for dd in range(2):
            wup[l, 0:64, dd * 256:(dd + 1) * 256] = inp["rwkv_w_up"][l, dd]
            wup[l, 64:128, dd * 256:(dd + 1) * 256] = inp["rwkv_a_up"][l, dd]
        for ct in range(2):
            ch = slice(ct * 128, (ct + 1) * 128)
            for q in range(4):
                colp[l, :, ct * 11 + q] = inp["lru_conv_w"][l, q, ch]
            colp[l, :, ct * 11 + 4] = inp["lru_conv_b"][l, ch]
            for dd in range(2):
                colp[l, :, ct * 11 + 5 + dd] = inp["lru_ba"][l, dd, ch]
                colp[l, :, ct * 11 + 7 + dd] = inp["lru_bx"][l, dd, ch]
                colp[l, :, ct * 11 + 9 + dd] = inp["lru_lambda"][l, dd, ch]
                for gate, wkey in ((0, "lru_wa"), (1, "lru_wx")):
                    for a in range(2):
                        bd[l, dd * 4 + gate * 2 + ct, a * 64:(a + 1) * 64, a * 64:(a + 1) * 64] = inp[wkey][l, dd, 2 * ct + a]
    sh["wup"], sh["colp"], sh["bd"] = wup, colp, bd
    colA = np.zeros((L, 128, 30), np.float32)
    for l in range(L):
        for ct in range(2):
            ch = slice(ct * 128, (ct + 1) * 128)
            for dd in range(2):
                colA[l, :, ct * 6 + dd] = inp["rwkv_w0"][l, dd, ch]
                colA[l, :, ct * 6 + 2 + dd] = inp["rwkv_a0"][l, dd, ch]
            colA[l, :, ct * 6 + 4] = inp["rwkv_k_k"][l, ch]
            colA[l, :, ct * 6 + 5] = inp["rwkv_k_a"][l, ch]
            colA[l, :, 24 + ct * 3 + 0] = inp["rwkv_gn_g"][l, ch]
            colA[l, :, 24 + ct * 3 + 1] = inp["rwkv_gn_b"][l, ch]
            colA[l, :, 24 + ct * 3 + 2] = np.asarray(inp["rwkv_r_k"][l]).reshape(256)[ch]
        for h in range(4):
            hc = slice(h * 64, (h + 1) * 64)
            for half in range(2):
                ps_ = slice(half * 64, (half + 1) * 64)
                colA[l, ps_, 12 + h * 3 + 0] = inp["rwkv_gn_g"][l, hc]
                colA[l, ps_, 12 + h * 3 + 1] = inp["rwkv_gn_b"][l, hc]
                colA[l, ps_, 12 + h * 3 + 2] = np.asarray(inp["rwkv_r_k"][l, h])
    sh["colA"] = colA
    sh.update(_consts())
    return sh


def _core_inputs(inp, sh, core):
    f = lambda a: np.ascontiguousarray(np.asarray(a, dtype=np.float32))
    sbi = core // 2
    m = dict(sh)
    m["xp"] = f(inp["x_prompt"][2 * core:2 * core + 2])
    m["xs"] = f(inp["x_sample"][sbi])
    cv = np.stack([np.asarray(inp["c_ctx"]), np.asarray(inp["c"][sbi])], 0)
    m["cvT"] = f(cv.reshape(2, 8, 128).transpose(0, 2, 1))
    m["cwk"] = f(np.asarray(inp["cache_win_k"][sbi]).reshape(L, 256, 128))
    m["cwv"] = f(np.asarray(inp["cache_win_v"][sbi]).reshape(L, 256, 128))
    m["cdk"] = f(np.asarray(inp["cache_diff_k"][sbi]).reshape(L, 256, 256))
    m["cdv"] = f(np.asarray(inp["cache_diff_v"][sbi]).reshape(L, 256, 256))
    m["srw"] = f(inp["state_rwkv"][sbi])
    m["srwT"] = f(np.asarray(inp["state_rwkv"][sbi]).transpose(0, 1, 2, 4, 3))
    slr = np.asarray(inp["state_lru"][sbi])
    m["slrT"] = f(slr.reshape(L, 2, 2, 128).transpose(0, 3, 2, 1).reshape(L, 128, 4))
    return m


_NC_CACHE = {}


def kernel(**inputs):
    inp = {k_: np.asarray(v) for k_, v in inputs.items()}
    if "nc" not in _NC_CACHE:
        _NC_CACHE["nc"] = build()[0]
    nc = _NC_CACHE["nc"]
    sh = _prep_shared(inp)
    in_maps = [_core_inputs(inp, sh, c) for c in range(8)]
    res = run_bass_kernel_spmd(nc, in_maps, core_ids=list(range(8)))
    R = res.results
    y_p = np.concatenate([R[c]["yp"] for c in range(8)], 0)
    y_s = np.stack([R[2 * b]["ys"] for b in range(4)], 0)
    cat = lambda name: np.concatenate([R[c][name] for c in range(8)], 0)
    nwk = cat("nwk").reshape(16, L, 256, 2, 64)
    nwv = cat("nwv").reshape(16, L, 256, 2, 64)
    ndk = cat("ndk").reshape(16, L, 256, 4, 2, 32)
    ndv = cat("ndv").reshape(16, L, 256, 4, 64)
    nsr = cat("nsr")
    nsl = cat("nsl").reshape(16, L, 128, 2, 2).transpose(0, 1, 4, 3, 2).reshape(16, L, 2, 256)
    out = (y_p, y_s, nwk, nwv, ndk, ndv, nsr, nsl)
    return tuple(np.ascontiguousarray(o, dtype=np.float32) for o in out)
```
